# Optimizing a Trainium2 kernel written in Bass

```python
import jax, jax.numpy as jnp
from jax import lax
import numpy as np

D_MODEL = 1024
BATCH = 16
SEQ = 2048
DEPTH = 2
DEC_BATCH = 4
DEC_SEQ = 8192
PAST_LEN = 128

HEAD_DIM = 64
A_HEADS = 8
A_KV_HEADS = 2
A_GROUP = A_HEADS // A_KV_HEADS
A_RADIUS = 128
B_HEADS = 8
B_CONFIGS = ((128, 1), (512, 4), (2048, 16))
M_HEADS = 4
M_HEAD_DIM = 128
N_MEM = 256
N_BRANCH = 3
BRANCH_WIDTH = 512
D_FF = 2816
LN_EPS = 1e-5
ALPHA = (2 * DEPTH) ** 0.25
BETA = (8 * DEPTH) ** -0.25
NEG_INF = -1e30

QA_COLS = A_HEADS * HEAD_DIM
KVA_COLS = 2 * A_KV_HEADS * HEAD_DIM
QKVB_COLS = 3 * B_HEADS * HEAD_DIM
QM_COLS = M_HEADS * M_HEAD_DIM
GATE_COLS = N_BRANCH * D_MODEL
IN_COLS = QA_COLS + KVA_COLS + QKVB_COLS + QM_COLS + GATE_COLS

kernel_name = "hybrid_bidir_encoder"


def _alibi_slopes(n):
    return jnp.asarray(2.0 ** (-8.0 * np.arange(1, n + 1) / n), dtype=jnp.float32)


def _layer_norm(x, g, b):
    xf = x.astype(jnp.float32)
    mu = xf.mean(-1, keepdims=True)
    var = jnp.square(xf - mu).mean(-1, keepdims=True)
    return ((xf - mu) * lax.rsqrt(var + LN_EPS) * g.astype(jnp.float32) + b.astype(jnp.float32)).astype(x.dtype)


def _swiglu(x, w_in, w_out):
    gate, up = jnp.split(x @ w_in, 2, axis=-1)
    return (jax.nn.silu(gate) * up) @ w_out


def _banded_attention(q, k, v, radius, slopes, dist_scale, sink=None):
    bsz, length, n_kv, grp, dh = q.shape
    blk = radius
    nb = -(-length // blk)
    pad = nb * blk - length
    qb = jnp.pad(q, ((0, 0), (0, pad), (0, 0), (0, 0), (0, 0))).reshape(bsz, nb, blk, n_kv, grp, dh)

    def _blocks(t):
        tp = jnp.pad(t, ((0, 0), (blk, pad + blk), (0, 0), (0, 0))).reshape(bsz, nb + 2, blk, n_kv, dh)
        return jnp.concatenate([tp[:, :-2], tp[:, 1:-1], tp[:, 2:]], axis=2)

    kb, vb = _blocks(k), _blocks(v)
    s = jnp.einsum('bnqhgd,bnkhd->bnhgqk', qb, kb, preferred_element_type=jnp.float32) * (dh ** -0.5)
    qi = jnp.arange(blk)[:, None]
    ki = jnp.arange(3 * blk)[None, :]
    rel = ki - blk - qi
    dist = jnp.abs(rel).astype(jnp.float32) * dist_scale
    key_pos = jnp.arange(nb)[:, None, None] * blk - blk + ki[None]
    valid = (jnp.abs(rel)[None] <= radius) & (key_pos >= 0) & (key_pos < length)
    s = s - slopes[:, :, None, None] * dist
    s = jnp.where(valid[None, :, None, None], s, NEG_INF)
    m = s.max(-1)
    if sink is not None:
        m = jnp.maximum(m, sink[:, :, None])
    p = jnp.exp(s - m[..., None])
    denom = p.sum(-1)
    total = denom + jnp.exp(sink[:, :, None] - m) if sink is not None else denom
    o = jnp.einsum('bnhgqk,bnkhd->bnqhgd', p, vb.astype(jnp.float32))
    o = o / total.transpose(0, 1, 4, 2, 3)[..., None]
    o = o.reshape(bsz, nb * blk, n_kv, grp, dh)[:, :length]
    lse = (m + jnp.log(denom)).transpose(0, 1, 4, 2, 3).reshape(bsz, nb * blk, n_kv, grp)[:, :length]
    return o, lse


def _dilated_attention(q, k, v, slopes):
    bsz, seq, nh, dh = q.shape
    outs, lses = [], []
    for window, dil in B_CONFIGS:
        radius = window // (2 * dil)
        sub_len = seq // dil

        def _sub(t):
            return t.reshape(bsz, sub_len, dil, nh, dh).transpose(0, 2, 1, 3, 4).reshape(bsz * dil, sub_len, nh, dh)

        o, lse = _banded_attention(_sub(q)[:, :, :, None], _sub(k), _sub(v), radius, slopes[:, None], dil)
        outs.append(o.reshape(bsz, dil, sub_len, nh, dh).transpose(0, 2, 1, 3, 4).reshape(bsz, seq, nh, dh))
        lses.append(lse.reshape(bsz, dil, sub_len, nh).transpose(0, 2, 1, 3).reshape(bsz, seq, nh))
    w = jax.nn.softmax(jnp.stack(lses), axis=0)
    return jnp.einsum('cbsh,cbshd->bshd', w, jnp.stack(outs))


def _memory_attention(q, mem, w_mem_kv):
    bsz, n_mem, _ = mem.shape
    km, vm = jnp.split(mem @ w_mem_kv, 2, axis=-1)
    km = km.reshape(bsz, n_mem, M_HEADS, M_HEAD_DIM)
    vm = vm.reshape(bsz, n_mem, M_HEADS, M_HEAD_DIM)
    s = jnp.einsum('bshd,bmhd->bhsm', q, km, preferred_element_type=jnp.float32) * (M_HEAD_DIM ** -0.5)
    p = jax.nn.softmax(s, axis=-1)
    return jnp.einsum('bhsm,bmhd->bshd', p, vm.astype(jnp.float32))


def _token_mix(x, mem, w_in, w_mem_kv, sink_a, w_branch, w_out):
    bsz, seq, _ = x.shape
    h = x @ w_in
    q_a, kv_a, qkv_b, q_m, gates = jnp.split(
        h, [QA_COLS, QA_COLS + KVA_COLS, QA_COLS + KVA_COLS + QKVB_COLS, QA_COLS + KVA_COLS + QKVB_COLS + QM_COLS], axis=-1)
    k_a, v_a = jnp.split(kv_a.reshape(bsz, seq, 2, A_KV_HEADS, HEAD_DIM), 2, axis=2)
    o_a, _ = _banded_attention(q_a.reshape(bsz, seq, A_KV_HEADS, A_GROUP, HEAD_DIM), k_a[:, :, 0], v_a[:, :, 0], A_RADIUS,
                               _alibi_slopes(A_HEADS).reshape(A_KV_HEADS, A_GROUP), 1,
                               sink=sink_a.astype(jnp.float32).reshape(A_KV_HEADS, A_GROUP))
    qkv_b = qkv_b.reshape(bsz, seq, 3, B_HEADS, HEAD_DIM)
    o_b = _dilated_attention(qkv_b[:, :, 0], qkv_b[:, :, 1], qkv_b[:, :, 2], _alibi_slopes(B_HEADS))
    o_m = _memory_attention(q_m.reshape(bsz, seq, M_HEADS, M_HEAD_DIM), mem, w_mem_kv)
    branches = jnp.stack([o_a.reshape(bsz, seq, BRANCH_WIDTH), o_b.reshape(bsz, seq, BRANCH_WIDTH),
                          o_m.reshape(bsz, seq, BRANCH_WIDTH)], axis=2).astype(x.dtype)
    proj = jnp.einsum('bsie,ied->bsid', branches, w_branch)
    g = jax.nn.sigmoid(gates).reshape(bsz, seq, N_BRANCH, D_MODEL)
    return (g * proj).sum(axis=2) @ w_out


def setup_inputs(seed: int = 0) -> dict:
    key = jax.random.key(seed)
    ks = jax.random.split(key, 20)
    f32 = jnp.float32

    def nrm(k, shape, scale):
        return jax.random.normal(k, shape, f32) * scale

    return {
        "x_prompt": nrm(ks[0], (BATCH, SEQ, D_MODEL), 1.0),
        "x_sample": nrm(ks[1], (DEC_BATCH, DEC_SEQ, D_MODEL), 1.0),
        "mem_prompt": nrm(ks[2], (BATCH, N_MEM, D_MODEL), 1.0),
        "mem_sample": nrm(ks[3], (DEC_BATCH, N_MEM, D_MODEL), 1.0),
        "ffn1_w_in": nrm(ks[4], (DEPTH, D_MODEL, 2 * D_FF), D_MODEL ** -0.5),
        "ffn1_w_out": nrm(ks[5], (DEPTH, D_FF, D_MODEL), BETA * D_FF ** -0.5),
        "ln1_g": 1.0 + nrm(ks[6], (DEPTH, D_MODEL), 0.02),
        "ln1_b": nrm(ks[7], (DEPTH, D_MODEL), 0.02),
        "w_in": nrm(ks[8], (DEPTH, D_MODEL, IN_COLS), D_MODEL ** -0.5),
        "w_mem_kv": nrm(ks[9], (DEPTH, D_MODEL, 2 * QM_COLS), D_MODEL ** -0.5),
        "sink_a": nrm(ks[10], (DEPTH, A_HEADS), 1.0),
        "w_branch": nrm(ks[11], (DEPTH, N_BRANCH, BRANCH_WIDTH, D_MODEL), BRANCH_WIDTH ** -0.5),
        "w_out": nrm(ks[12], (DEPTH, D_MODEL, D_MODEL), BETA * D_MODEL ** -0.5),
        "ln2_g": 1.0 + nrm(ks[13], (DEPTH, D_MODEL), 0.02),
        "ln2_b": nrm(ks[14], (DEPTH, D_MODEL), 0.02),
        "ffn2_w_in": nrm(ks[15], (DEPTH, D_MODEL, 2 * D_FF), D_MODEL ** -0.5),
        "ffn2_w_out": nrm(ks[16], (DEPTH, D_FF, D_MODEL), BETA * D_FF ** -0.5),
        "ln3_g": 1.0 + nrm(ks[17], (DEPTH, D_MODEL), 0.02),
        "ln3_b": nrm(ks[18], (DEPTH, D_MODEL), 0.02),
    }


def reference(x_prompt, x_sample, mem_prompt, mem_sample, ffn1_w_in, ffn1_w_out, ln1_g, ln1_b, w_in, w_mem_kv,
              sink_a, w_branch, w_out, ln2_g, ln2_b, ffn2_w_in, ffn2_w_out, ln3_g, ln3_b):
    def trunk(x, mem):
        for l in range(DEPTH):
            x = _layer_norm(ALPHA * x + 0.5 * _swiglu(x, ffn1_w_in[l], ffn1_w_out[l]), ln1_g[l], ln1_b[l])
            x = _layer_norm(ALPHA * x + _token_mix(x, mem, w_in[l], w_mem_kv[l], sink_a[l], w_branch[l], w_out[l]),
                            ln2_g[l], ln2_b[l])
            x = _layer_norm(ALPHA * x + 0.5 * _swiglu(x, ffn2_w_in[l], ffn2_w_out[l]), ln3_g[l], ln3_b[l])
        return x

    y_prompt = trunk(x_prompt, mem_prompt)
    y_sample = trunk(x_sample, mem_sample)
    return (y_prompt, y_sample)
```

```python
from contextlib import ExitStack

import numpy as np
import ml_dtypes
import concourse.bass as bass
import concourse.mybir as mybir
from concourse.bass_utils import run_bass_kernel_spmd

F32 = mybir.dt.float32
BF16 = mybir.dt.bfloat16
AF = mybir.ActivationFunctionType
ALU = mybir.AluOpType

SEG = 2048
D = 1024
DFF = 2816
ALPHA = 4.0 ** 0.25
LN_EPS = 1e-5
EPS2 = LN_EPS / (ALPHA * ALPHA)
ENGS = ("pe", "act", "dve", "pool", "sp")
NSEM_POOL = 96


class Res:
    __slots__ = ("lw", "rd")

    def __init__(self):
        self.lw = None
        self.rd = []


class Op:
    __slots__ = ("eng", "fn", "deps", "sig", "event", "dmakey", "ndma", "done")


class Sched:
    def __init__(self, nc, es):
        self.nc = nc
        self.pending = {e: [] for e in ENGS}
        self.allp = []
        self.cnt = {}
        self.waited = {e: {} for e in ENGS}
        self.sempool = [es.enter_context(nc.semaphore("sm%d" % i)) for i in range(NSEM_POOL)]
        self.semmap = {}
        self.nops = 0

    def sem(self, k):
        if k not in self.semmap:
            assert len(self.semmap) < NSEM_POOL, "out of semaphores"
            self.semmap[k] = self.sempool[len(self.semmap)]
        return self.semmap[k]

    def op(self, eng, fn, reads=(), writes=(), dmakey=None, ndma=1, weak=()):
        o = Op()
        o.eng, o.fn, o.dmakey, o.ndma = eng, fn, dmakey, ndma
        o.sig = False
        o.event = None
        o.done = False
        deps = set()
        for r in reads:
            if r.lw is not None and not r.lw.done:
                deps.add(r.lw)
        for w in writes:
            live = [x for x in w.rd if not x.done]
            if w.lw is not None and not w.lw.done and not live:
                deps.add(w.lw)
            deps.update(live)
        deps.discard(o)
        if fn is not None:
            for r in reads:
                r.rd.append(o)
        for w in writes:
            w.lw = o
            w.rd = []
        for w in weak:
            w.lw = o
        o.deps = deps
        self.pending[eng].append(o)
        self.allp.append(o)
        return o

    def dma(self, fn, key, reads=(), writes=(), ndma=1, eng="sp"):
        return self.op(eng, fn, reads, writes, dmakey=key, ndma=ndma)

    def ready(self, engs, res_list):
        for e in engs:
            self.op(e, None, reads=res_list)

    @staticmethod
    def _skip(o, d):
        return d.dmakey is None and o.dmakey is None and d.eng == "pe" and o.eng == "pe"

    def barrier_and_emit(self):
        lasts = []
        for e in ENGS:
            comp = [o for o in self.pending[e] if o.dmakey is None and o.fn is not None]
            if comp:
                lasts.append(comp[-1])
        lastkey = {}
        for o in self.allp:
            if o.dmakey is not None:
                lastkey[o.dmakey] = o
        lasts += list(lastkey.values())
        for e in ENGS:
            b = self.op(e, None)
            b.deps = set(lasts)
        self._emit()

    def _emit(self):
        nc = self.nc
        skip = self._skip
        for o in self.allp:
            for d in o.deps:
                if d.dmakey is None and not skip(o, d):
                    d.sig = True
        for o in self.allp:
            if o.dmakey is not None:
                k = ("dma", o.dmakey)
                self.cnt[k] = self.cnt.get(k, 0) + 16 * o.ndma
                o.event = (k, self.cnt[k])
            elif o.sig:
                k = ("eng", o.eng)
                self.cnt[k] = self.cnt.get(k, 0) + 1
                o.event = (k, self.cnt[k])
        with nc.Block() as block:
            def run(engname, e):
                waited = self.waited[engname]
                for o in self.pending[engname]:
                    need = {}
                    for d in o.deps:
                        if skip(o, d):
                            continue
                        k, v = d.event
                        if need.get(k, 0) < v:
                            need[k] = v
                    for k, v in need.items():
                        if waited.get(k, 0) < v:
                            e.wait_ge(self.sem(k), v)
                            waited[k] = v
                    if o.fn is None:
                        continue
                    r = o.fn(e)
                    self.nops += 1
                    if o.dmakey is not None:
                        if not isinstance(r, (list, tuple)):
                            r = [r]
                        assert len(r) == o.ndma
                        for ins in r:
                            ins.then_inc(self.sem(o.event[0]), 16)
                    elif o.sig:
                        if isinstance(r, (list, tuple)):
                            r = r[-1]
                        r.then_inc(self.sem(o.event[0]), 1)

            if self.pending["pe"]:
                @block.tensor
                def _(e):
                    run("pe", e)
            if self.pending["act"]:
                @block.scalar
                def _(e):
                    run("act", e)
            if self.pending["dve"]:
                @block.vector
                def _(e):
                    run("dve", e)
            if self.pending["pool"]:
                @block.gpsimd
                def _(e):
                    run("pool", e)
            if self.pending["sp"]:
                @block.sync
                def _(e):
                    run("sp", e)
        for o in self.allp:
            o.done = True
            o.deps = None
            o.fn = None
        self.pending = {e: [] for e in ENGS}
        self.allp = []


class Rot:
    def __init__(self, tiles):
        self.tiles = tiles
        self.res = [Res() for _ in tiles]
        self.i = 0

    def next(self):
        i = self.i % len(self.tiles)
        self.i += 1
        return self.tiles[i], self.res[i], i


def MM(lst):
    def fn(e):
        r = None
        for t in lst:
            o, l, rh, st, sp = t[:5]
            if len(t) > 5 and t[5]:
                r = e.matmul(o, lhsT=l, rhs=rh, start=st, stop=sp, skip_group_check=True)
            else:
                r = e.matmul(o, lhsT=l, rhs=rh, start=st, stop=sp)
        return r
    return fn


def TR(lst):
    def fn(e):
        r = None
        for (o, i, ident) in lst:
            r = e.transpose(out=o, in_=i, identity=ident)
        return r
    return fn


def ACTF(out, in_, func, scale=1.0):
    return lambda e: e.activation(out=out, in_=in_, func=func, scale=scale)


def COPY(eng, out, in_):
    if eng == "act":
        return lambda e: e.copy(out=out, in_=in_)
    return lambda e: e.tensor_copy(out=out, in_=in_)


def TT(out, in0, in1, op):
    return lambda e: e.tensor_tensor(out=out, in0=in0, in1=in1, op=op)


def TS(out, in0, s1, s2, op0, op1=None):
    if op1 is None:
        return lambda e: e.tensor_scalar(out=out, in0=in0, scalar1=s1, scalar2=None, op0=op0)
    return lambda e: e.tensor_scalar(out=out, in0=in0, scalar1=s1, scalar2=s2, op0=op0, op1=op1)


def STT(out, in0, scalar, in1, op0, op1):
    return lambda e: e.scalar_tensor_tensor(out=out, in0=in0, scalar=scalar, in1=in1, op0=op0, op1=op1)


def MEMSET(ap, v):
    return lambda e: e.memset(ap, v)


def RECIP(out, in_):
    return lambda e: e.reciprocal(out=out, in_=in_)


def recip_act(S, out, in_, reads, res, weak=()):
    S.op("act", ACTF(out, in_, AF.Ln), reads=reads, writes=[res])
    S.op("act", ACTF(out, out, AF.Exp, scale=-1.0), reads=[res], writes=[res], weak=weak)


def DMA(out, in_):
    return lambda e: e.dma_start(out=out, in_=in_)


class Ctx:
    pass


_uid = [0]


def alloc(nc, es, kind, name, shape, dt):
    _uid[0] += 1
    name = "%s_%d" % (name, _uid[0])
    if kind == "sb":
        return es.enter_context(nc.sbuf_tensor(name, shape, dt))
    return es.enter_context(nc.psum_tensor(name, shape, dt))


def load_weights_cast(S, pieces, key="w"):
    rl = []
    for dst, src in pieces:
        r = Res()
        S.dma(DMA(dst, src), key, writes=[r], eng="pool")
        rl.append(r)
    return rl


def load_bcast_row(S, nc, dst, src_row, key):
    r = Res()
    S.dma(DMA(dst, src_row.partition_broadcast(128)), key, writes=[r])
    return r


def end_phase_a(S, C, yparts, r_yh, res_src, coef, slot):
    xs_, r_xs, si = slot
    if res_src is not None:
        S.dma(DMA(xs_[:, :], res_src), ("xsl", si), writes=[r_xs])
    for h in range(2):
        S.op("dve", STT(xs_[:, h * 512:(h + 1) * 512], yparts[h], coef,
                        xs_[:, h * 512:(h + 1) * 512], ALU.mult, ALU.add),
             reads=[r_yh[h]], writes=[r_xs])


def end_phase_b(S, C, out_dst, slot):
    xs_, r_xs, si = slot
    mv = C.mv[si]
    st = C.st[si]
    r_mv = C.r_mv[si]
    S.op("dve", lambda e: e.bn_stats(out=st[:, 0:6], in_=xs_[:, 0:512]), reads=[r_xs], writes=[r_mv])
    S.op("dve", lambda e: e.bn_stats(out=st[:, 6:12], in_=xs_[:, 512:1024]), reads=[r_xs], writes=[r_mv])
    S.op("dve", lambda e: e.bn_aggr(out=mv[:, 0:2], in_=st[:, 0:12]), reads=[r_mv], writes=[r_mv])
    S.op("dve", TS(mv[:, 2:3], mv[:, 1:2], EPS2, None, ALU.add), reads=[r_mv], writes=[r_mv])
    S.op("act", ACTF(mv[:, 3:4], mv[:, 2:3], AF.Ln), reads=[r_mv], writes=[r_mv])
    S.op("act", ACTF(mv[:, 4:5], mv[:, 3:4], AF.Exp, scale=-0.5), reads=[r_mv], writes=[r_mv])
    S.op("dve", TS(mv[:, 5:6], mv[:, 0:1], mv[:, 4:5], -1.0, ALU.mult, ALU.mult), reads=[r_mv], writes=[r_mv])
    S.op("act", lambda e: e.activation(out=xs_[:, :], in_=xs_[:, :], func=AF.Identity, bias=mv[:, 5:6], scale=mv[:, 4:5]),
         reads=[r_mv, r_xs], writes=[r_xs])
    S.op("pool", TT(xs_[:, :], xs_[:, :], C.gt[:, :], ALU.mult), reads=[r_xs], writes=[r_xs])
    S.op("pool", TT(xs_[:, :], xs_[:, :], C.bt[:, :], ALU.add), reads=[r_xs], writes=[r_xs])
    S.dma(DMA(out_dst, xs_[:, :]), ("xss", si), reads=[r_xs], writes=[Res()])


def end_phase_b_gen(S, C, out_dst, slot):
    xs_, r_xs, si = slot
    mv = C.mv[si]
    st = C.st[si]
    r_mv = C.r_mv[si]
    S.op("dve", lambda e: e.bn_stats(out=st[:, 0:6], in_=xs_[:, 0:512]), reads=[r_xs], writes=[r_mv])
    yield
    S.op("dve", lambda e: e.bn_stats(out=st[:, 6:12], in_=xs_[:, 512:1024]), reads=[r_xs], writes=[r_mv])
    yield
    S.op("dve", lambda e: e.bn_aggr(out=mv[:, 0:2], in_=st[:, 0:12]), reads=[r_mv], writes=[r_mv])
    S.op("dve", TS(mv[:, 2:3], mv[:, 1:2], EPS2, None, ALU.add), reads=[r_mv], writes=[r_mv])
    yield
    S.op("act", ACTF(mv[:, 3:4], mv[:, 2:3], AF.Ln), reads=[r_mv], writes=[r_mv])
    S.op("act", ACTF(mv[:, 4:5], mv[:, 3:4], AF.Exp, scale=-0.5), reads=[r_mv], writes=[r_mv])
    yield
    S.op("dve", TS(mv[:, 5:6], mv[:, 0:1], mv[:, 4:5], -1.0, ALU.mult, ALU.mult), reads=[r_mv], writes=[r_mv])
    S.op("act", lambda e: e.activation(out=xs_[:, :], in_=xs_[:, :], func=AF.Identity, bias=mv[:, 5:6], scale=mv[:, 4:5]),
         reads=[r_mv, r_xs], writes=[r_xs])
    yield
    S.op("pool", TT(xs_[:, :], xs_[:, :], C.gt[:, :], ALU.mult), reads=[r_xs], writes=[r_xs])
    yield
    S.op("pool", TT(xs_[:, :], xs_[:, :], C.bt[:, :], ALU.add), reads=[r_xs], writes=[r_xs])
    yield
    S.dma(DMA(out_dst, xs_[:, :]), ("xss", si), reads=[r_xs], writes=[Res()])


def end_phase(S, C, yparts, r_yh, res_src, out_dst, coef, slot):
    end_phase_a(S, C, yparts, r_yh, res_src, coef, slot)
    end_phase_b(S, C, out_dst, slot)


def prefetch_res(S, C, src):
    slot = C.xs.next()
    xs_, r_xs, si = slot
    S.dma(DMA(xs_[:, :], src), ("xsl", si), writes=[r_xs])
    return slot


def alloc_end_phase(nc, es, C, nslots=4):
    C.xs = Rot([alloc(nc, es, "sb", "xs%d" % i, [128, 1024], F32) for i in range(nslots)])
    C.mv = [alloc(nc, es, "sb", "mv%d" % i, [128, 8], F32) for i in range(nslots)]
    C.st = [alloc(nc, es, "sb", "st%d" % i, [128, 12], F32) for i in range(nslots)]
    C.r_mv = [Res() for _ in range(nslots)]
    C.gt = alloc(nc, es, "sb", "gt", [128, 1024], F32)
    C.bt = alloc(nc, es, "sb", "bt", [128, 1024], F32)


def load_x_transposed(S, C, x_src_rows, xT_dst, r_dst, nsub):
    for s in range(nsub):
        xb, r_xb, bi = C.xb.next()
        S.dma(DMA(xb[:, :], x_src_rows(s)), ("xb", bi), writes=[r_xb], eng="pool")
        pt, r_pt, _ = C.pT.next()
        S.op("pe", TR([(pt[:, c, :], xb[:, c * 128:(c + 1) * 128], C.ident[:, :]) for c in range(8)]),
             reads=[r_xb], writes=[r_pt])
        S.op("act", COPY("act", xT_dst[:, :, s * 128:(s + 1) * 128], pt[:, :, :]), reads=[r_pt], writes=[r_dst[s]])


def stage_ffn(nc, S, G, x_in, x_out, w_in, w_out, g_row, b_row, TF=256):
    NT = G.NTOK // TF
    NS = TF // 128
    with ExitStack() as es:
        C = Ctx()
        Win = alloc(nc, es, "sb", "Win", [128, 8, 2 * DFF], BF16)
        Wout = alloc(nc, es, "sb", "Wout", [128, 22, 1024], BF16)
        C.ident = alloc(nc, es, "sb", "ident", [128, 128], BF16)
        alloc_end_phase(nc, es, C, 4)
        C.xb = Rot([alloc(nc, es, "sb", "xb%d" % i, [128, 1024], BF16) for i in range(4)])
        xT = Rot([alloc(nc, es, "sb", "xT%d" % i, [128, 8, TF], BF16) for i in range(2)])
        r_xTs = [[Res() for _ in range(NS)] for _ in range(2)]
        hT = alloc(nc, es, "sb", "hT", [128, 22, TF], BF16)
        r_hT = [Res() for _ in range(22)]
        sg = Rot([alloc(nc, es, "sb", "sg%d" % i, [128, TF], F32) for i in range(3)])
        C.pT = Rot([alloc(nc, es, "ps", "pT%d" % i, [128, 8, 128], BF16) for i in range(1)])
        gpb = Rot([alloc(nc, es, "ps", "gp%d" % i, [128, 512], F32) for i in range(2)])
        upb = Rot([alloc(nc, es, "ps", "up%d" % i, [128, 512], F32) for i in range(2)])
        yps = Rot([alloc(nc, es, "ps", "yp%d" % i, [128, 1024], F32) for i in range(1)])
        r_yh = [[Res(), Res()] for _ in range(len(yps.tiles))]

        r_c = [Res()]
        S.dma(DMA(C.ident[:, :], G.ident_d), "cst", writes=[r_c[0]])
        r_c.append(load_bcast_row(S, nc, C.gt[:, :], g_row, "cst2"))
        r_c.append(load_bcast_row(S, nc, C.bt[:, :], b_row, "cst3"))
        S.ready(["pe"], r_c)
        S.ready(["pool", "dve", "act"], r_c)

        def prep(t):
            t0_ = t * TF
            xTt_, _, xi_ = xT.next()
            load_x_transposed(S, C, lambda s: x_in[t0_ + s * 128:t0_ + (s + 1) * 128, :], xTt_, r_xTs[xi_], NS)
            return xTt_, xi_

        nxt = prep(0)
        WAVES = [(0, 6), (6, 12), (12, 22)]
        r_wave = []
        for wi_, (c0_, c1_) in enumerate(WAVES):
            pieces = []
            for kc in range(8):
                rows = slice(kc * 128, (kc + 1) * 128)
                pieces.append((Win[:, kc, c0_ * 128:c1_ * 128], w_in[rows, c0_ * 128:c1_ * 128]))
                pieces.append((Win[:, kc, DFF + c0_ * 128:DFF + c1_ * 128], w_in[rows, DFF + c0_ * 128:DFF + c1_ * 128]))
            r_wave.append(load_weights_cast(S, pieces, key="w%d" % wi_))
        r_wout = load_weights_cast(S, [(Wout[:, ch, :], w_out[ch * 128:(ch + 1) * 128, :]) for ch in range(22)], key="w3")
        wave_joined = [False] * 4
        for t in range(NT):
            t0 = t * TF
            xTt, xi = nxt
            rslots = [prefetch_res(S, C, x_in[t0 + s * 128:t0 + (s + 1) * 128, :]) for s in range(NS)]
            for ch in range(22):
                for wi_, (c0_, c1_) in enumerate(WAVES):
                    if ch == c0_ and not wave_joined[wi_]:
                        S.ready(["pe"], r_wave[wi_])
                        wave_joined[wi_] = True
                gt_, r_g, _ = gpb.next()
                ut_, r_u, _ = upb.next()
                g_ = gt_[:, 0:TF]
                u_ = ut_[:, 0:TF]
                S.op("pe", MM([(g_, Win[:, kc, ch * 128:(ch + 1) * 128], xTt[:, kc, :], kc == 0, kc == 7)
                               for kc in range(8)]), reads=r_xTs[xi], writes=[r_g])
                S.op("pe", MM([(u_, Win[:, kc, DFF + ch * 128:DFF + (ch + 1) * 128], xTt[:, kc, :], kc == 0, kc == 7)
                               for kc in range(8)]), reads=r_xTs[xi], writes=[r_u])
                sg_, r_sg, _ = sg.next()
                S.op("act", ACTF(sg_[:, :], g_, AF.Silu), reads=[r_g], writes=[r_sg])
                S.op("dve", TT(hT[:, ch, :], u_, sg_[:, :], ALU.mult), reads=[r_u, r_sg], writes=[r_hT[ch]])
                if ch == 10 and t + 1 < NT:
                    nxt = prep(t + 1)
            if not wave_joined[3]:
                S.ready(["pe"], r_wout)
                wave_joined[3] = True
            for s in range(NS):
                yp, _, yi = yps.next()
                for (c0_, c1_) in ((0, 11), (11, 22)):
                    for h in range(2):
                        S.op("pe", MM([(yp[:, h * 512:(h + 1) * 512], hT[:, ch, s * 128:(s + 1) * 128],
                                        Wout[:, ch, h * 512:(h + 1) * 512], ch == 0, ch == 21) for ch in range(c0_, c1_)]),
                             reads=r_hT[c0_:c1_], writes=[r_yh[yi][h]])
                rows = slice(t0 + s * 128, t0 + (s + 1) * 128)
                end_phase(S, C, [yp[:, 0:512], yp[:, 512:1024]], r_yh[yi], None, x_out[rows, :], 0.5 / ALPHA, rslots[s])
        S.barrier_and_emit()


def stage_proj(nc, S, G, x_in, w_in, T2=512):
    NT = G.NTOK // T2
    NS = T2 // 128
    NFM = 42
    with ExitStack() as es:
        C = Ctx()
        Wf = alloc(nc, es, "sb", "Wf", [128, 8, NFM * 128], BF16)
        Wv = alloc(nc, es, "sb", "Wv", [128, 8, 640], BF16)
        C.ident = alloc(nc, es, "sb", "ident", [128, 128], BF16)
        C.xb = Rot([alloc(nc, es, "sb", "xb%d" % i, [128, 1024], BF16) for i in range(4)])
        xT = Rot([alloc(nc, es, "sb", "xT%d" % i, [128, 8, T2], BF16) for i in range(2)])
        r_xTs = [[Res() for _ in range(NS)] for _ in range(2)]
        ost = Rot([alloc(nc, es, "sb", "ost%d" % i, [128, T2], BF16) for i in range(6)])
        vst = Rot([alloc(nc, es, "sb", "vst%d" % i, [128, 640], BF16) for i in range(3)])
        C.pT = Rot([alloc(nc, es, "ps", "pT%d" % i, [128, 8, 128], BF16) for i in range(1)])
        pf = Rot([alloc(nc, es, "ps", "pf%d" % i, [128, T2], F32) for i in range(3)])
        pv = Rot([alloc(nc, es, "ps", "pv%d" % i, [128, 512], F32) for i in range(2)])
        pv2 = Rot([alloc(nc, es, "ps", "pw%d" % i, [128, 512], F32) for i in range(1)])

        r_c = [Res()]
        S.dma(DMA(C.ident[:, :], G.ident_d), "cst", writes=[r_c[0]])
        S.ready(["pe"], r_c)
        first_x = xT.next()
        load_x_transposed(S, C, lambda s: x_in[s * 128:(s + 1) * 128, :], first_x[0], r_xTs[first_x[2]], NS)
        wv = [[], [], [], []]
        for kc in range(8):
            rows = slice(kc * 128, (kc + 1) * 128)
            wv[0].append((Wf[:, kc, 0:512], w_in[rows, 0:512]))
            for g in range(2):
                for dup in range(2):
                    c0 = (4 + g) * 128 + dup * 64
                    wv[0].append((Wf[:, kc, c0:c0 + 64], w_in[rows, 512 + 64 * g:512 + 64 * g + 64]))
            wv[1].append((Wf[:, kc, 6 * 128:14 * 128], w_in[rows, 768:1792]))
            wv[1].append((Wf[:, kc, 14 * 128:18 * 128], w_in[rows, 2304:2816]))
            wv[2].append((Wf[:, kc, 18 * 128:30 * 128], w_in[rows, 2816:2816 + 1536]))
            wv[2].append((Wf[:, kc, 30 * 128:42 * 128], w_in[rows, 2816 + 1536:5888]))
            pieces = wv[3]
            pieces.append((Wv[:, kc, 0:128], w_in[rows, 640:768]))
            for hh in range(8):
                dc = 128 + 128 * (hh // 2) + (0 if hh % 2 == 1 else 64)
                pieces.append((Wv[:, kc, dc:dc + 64], w_in[rows, 1792 + 64 * hh:1792 + 64 * hh + 64]))
        r_wave = [load_weights_cast(S, wv[i], key="w%d" % i) for i in range(4)]
        S.ready(["pe"], r_c)
        wave_joined = [False] * 4
        wave_start = {0: 0, 6: 1, 18: 2}

        def dst_of(oc):
            if oc < 4:
                return G.QAT, oc
            if oc < 6:
                return G.KAT, oc - 4
            if oc < 10:
                return G.QBT, oc - 6
            if oc < 14:
                return G.KBT, oc - 10
            if oc < 18:
                return G.QMT, oc - 14
            return G.GT, oc - 18

        evi = 0
        for t in range(NT):
            t0 = t * T2
            if t == 0:
                xTt, _, xi = first_x
            else:
                xTt, _, xi = xT.next()
                load_x_transposed(S, C, lambda s: x_in[t0 + s * 128:t0 + (s + 1) * 128, :], xTt, r_xTs[xi], NS)
            for oc in range(NFM):
                if oc in wave_start and not wave_joined[wave_start[oc]]:
                    S.ready(["pe"], r_wave[wave_start[oc]])
                    wave_joined[wave_start[oc]] = True
                p_, r_p, _ = pf.next()
                S.op("pe", MM([(p_[:, :], Wf[:, kc, oc * 128:(oc + 1) * 128], xTt[:, kc, :], kc == 0, kc == 7)
                               for kc in range(8)]), reads=r_xTs[xi], writes=[r_p])
                o_, r_o, oi = ost.next()
                if oc >= 18:
                    S.op("act", ACTF(o_[:, :], p_[:, :], AF.Sigmoid), reads=[r_p], writes=[r_o])
                else:
                    eng = "dve" if evi % 2 == 0 else "act"
                    evi += 1
                    S.op(eng, COPY(eng, o_[:, :], p_[:, :]), reads=[r_p], writes=[r_o])
                dt_, ci = dst_of(oc)
                S.dma(DMA(dt_[ci, :, t0:t0 + T2], o_[:, :]), ("ost", oi), reads=[r_o], writes=[Res()])
            if not wave_joined[3]:
                S.ready(["pe"], r_wave[3])
                wave_joined[3] = True
            for s in range(NS):
                a_, r_a, _ = pv.next()
                b_, r_b, _ = pv2.next()
                S.op("pe", MM([(a_[:, :], xTt[:, kc, s * 128:(s + 1) * 128], Wv[:, kc, 128:640], kc == 0, kc == 7)
                               for kc in range(8)]), reads=r_xTs[xi], writes=[r_a])
                S.op("pe", MM([(b_[:, 0:128], xTt[:, kc, s * 128:(s + 1) * 128], Wv[:, kc, 0:128], kc == 0, kc == 7)
                               for kc in range(8)]), reads=r_xTs[xi], writes=[r_b])
                v_, r_v, vi = vst.next()
                S.op("dve", COPY("dve", v_[:, 128:640], a_[:, :]), reads=[r_a], writes=[r_v])
                S.op("dve", COPY("dve", v_[:, 0:128], b_[:, 0:128]), reads=[r_b], writes=[r_v])
                S.dma(DMA(G.V[t0 + s * 128:t0 + (s + 1) * 128, :], v_[:, :]), ("vst", vi), reads=[r_v], writes=[Res()])
        S.barrier_and_emit()


WNAMES = [("ffn1_w_in", [D, 2 * DFF]), ("ffn1_w_out", [DFF, D]), ("ln1_g", [D]), ("ln1_b", [D]),
          ("w_in", [D, 5888]), ("w_mem_kv", [D, 1024]), ("sink_a", [8]), ("w_branch", [3, 512, D]),
          ("w_out", [D, D]), ("ln2_g", [D]), ("ln2_b", [D]), ("ffn2_w_in", [D, 2 * DFF]),
          ("ffn2_w_out", [DFF, D]), ("ln3_g", [D]), ("ln3_b", [D])]


def build(nseg=4, depth=2, upto=99, TF=256):
    nc = bass.Bass("TRN2", target_bir_lowering=False)
    G = Ctx()
    G.nseg = nseg
    G.NTOK = NTOK = nseg * SEG
    x = nc.dram_tensor("x", [NTOK, D], F32, kind="ExternalInput").ap()
    G.mem = nc.dram_tensor("mem", [nseg, 256, D], F32, kind="ExternalInput").ap()
    W = {}
    for nm, shp in WNAMES:
        W[nm] = nc.dram_tensor(nm, [depth] + shp, F32, kind="ExternalInput").ap()
    G.ident_d = nc.dram_tensor("ident", [128, 128], BF16, kind="ExternalInput").ap()
    G.EA_d = nc.dram_tensor("etab_a", [128, 10, 512], BF16, kind="ExternalInput").ap()
    G.EB_d = nc.dram_tensor("etab_b", [128, 12, 512], BF16, kind="ExternalInput").ap()
    G.sel_d = nc.dram_tensor("sel", [1, 256], F32, kind="ExternalInput").ap()
    y = nc.dram_tensor("y", [NTOK, D], F32, kind="ExternalOutput").ap()
    XA = nc.dram_tensor("XA", [NTOK, D], F32).ap()
    XB = nc.dram_tensor("XB", [NTOK, D], F32).ap()
    XC = nc.dram_tensor("XC", [NTOK, D], F32).ap()
    G.QAT = nc.dram_tensor("QAT", [4, 128, NTOK], BF16).ap()
    G.KAT = nc.dram_tensor("KAT", [2, 128, NTOK], BF16).ap()
    G.QBT = nc.dram_tensor("QBT", [4, 128, NTOK], BF16).ap()
    G.KBT = nc.dram_tensor("KBT", [4, 128, NTOK], BF16).ap()
    G.QMT = nc.dram_tensor("QMT", [4, 128, NTOK], BF16).ap()
    G.GT = nc.dram_tensor("GT", [24, 128, NTOK], BF16).ap()
    G.V = nc.dram_tensor("V", [NTOK, 640], BF16).ap()
    G.BRB = nc.dram_tensor("BRB", [4, 128, NTOK], BF16).ap()
    G.KMT = nc.dram_tensor("KMT", [nseg, 128, 4, 256], BF16).ap()
    G.VM = nc.dram_tensor("VM", [nseg, 256, 512], BF16).ap()
    nst = 0
    with ExitStack() as es:
        S = Sched(nc, es)
        cur = x
        for l in range(depth):
            last = (l == depth - 1)
            nst += 1
            if nst > upto:
                break
            stage_ffn(nc, S, G, cur, XA if upto > nst else y, W["ffn1_w_in"][l], W["ffn1_w_out"][l],
                      W["ln1_g"][l], W["ln1_b"][l], TF)
            nst += 1
            if nst > upto:
                break
            stage_proj(nc, S, G, XA, W["w_in"][l])
            nst += 1
            if nst > upto:
                break
            stage_attn_b(nc, S, G)
            nst += 1
            if nst > upto:
                break
            stage_memkv(nc, S, G, W["w_mem_kv"][l])
            stage_mix(nc, S, G, XA, XB if upto > nst else y, W["sink_a"][l], W["w_branch"][l],
                      W["w_out"][l], W["ln2_g"][l], W["ln2_b"][l])
            nst += 1
            if nst > upto:
                break
            stage_ffn(nc, S, G, XB, y if (last or upto <= nst) else XC, W["ffn2_w_in"][l], W["ffn2_w_out"][l],
                      W["ln3_g"][l], W["ln3_b"][l], TF)
            cur = XC
        G.nops = S.nops
        G.nsem = len(S.semmap)
    return nc, G


def host_tables(J):
    bf = ml_dtypes.bfloat16
    j = np.arange(128)[:, None].astype(np.float64)
    i = np.arange(128)[None, :].astype(np.float64)
    ea = np.zeros((128, 10, 512), np.float64)
    for g in range(2):
        for ty in range(5):
            off = [-1, 0, 1, -1, 1][ty]
            rel = off * 128 + j - i
            valid = (np.abs(rel) <= 128)
            for half in range(2):
                for c in range(2):
                    h = 2 * (2 * g + c) + half
                    slope = 2.0 ** (-(h + 1))
                    e = np.exp(-slope * np.abs(rel)) * valid
                    if ty >= 3 and J == 0:
                        e = e * 0.0
                    col = half * 256 + c * 128
                    ea[:, g * 5 + ty, col:col + 128] = e
    eb = np.zeros((128, 12, 512), np.float64)
    for xi in range(12):
        se = 2.0 ** (-(xi - 3))
        rel_prev = j - i + 64
        rel_cur = j - 64 - i
        e_prev = np.exp(-se * np.abs(rel_prev)) * (np.abs(rel_prev) <= 64)
        e_cur = np.exp(-se * np.abs(rel_cur)) * (np.abs(rel_cur) <= 64)
        eb[:, xi, 0:128] = e_prev
        eb[:, xi, 128:256] = e_cur
        lo = e_cur.copy()
        hi = e_prev.copy()
        if J == 0:
            lo[0:64, :] = 0.0
            hi[64:128, :] = 0.0
        eb[:, xi, 256:384] = lo
        eb[:, xi, 384:512] = hi
    sel = np.zeros((1, 256), np.float32)
    sel[0, 64:128] = 1.0
    sel[0, 128:192] = 1.0
    return {"ident": np.eye(128).astype(bf), "etab_a": ea.astype(np.float32).astype(bf),
            "etab_b": eb.astype(np.float32).astype(bf), "sel": sel}


KH = 1024


def stage_attn_b(nc, S, G):
    import os
    DBG = int(os.environ.get('ATTNB_DBG', '9'))
    NTOK, nseg = G.NTOK, G.nseg
    DILS = (1, 4, 16)
    with ExitStack() as es:
        EB = alloc(nc, es, "sb", "EB", [128, 12, 512], BF16)
        Qn = Rot([alloc(nc, es, "sb", "Qn%d" % i, [128, SEG], BF16) for i in range(2)])
        Kn = Rot([alloc(nc, es, "sb", "Kn%d" % i, [128, SEG + 2 * KH], BF16) for i in range(2)])
        Qd2 = [{4: alloc(nc, es, "sb", "Qd4", [128, 4, 512], BF16), 16: alloc(nc, es, "sb", "Qd16", [128, 16, 128], BF16)} for _ in range(2)]
        Kd2 = [{4: alloc(nc, es, "sb", "Kd4", [128, 4, 640], BF16), 16: alloc(nc, es, "sb", "Kd16", [128, 16, 256], BF16)} for _ in range(2)]
        r_Qd2 = [{4: Res(), 16: Res()} for _ in range(2)]
        r_Kd2 = [{4: Res(), 16: Res()} for _ in range(2)]
        Wall, Wt, r_Wt, wstate = {}, {}, {}, {}
        for d in DILS:
            nqb = SEG // (128 * d)
            Wall[d] = alloc(nc, es, "sb", "Wall%d" % d, [128, d * (nqb + 1), 256], BF16)
            for r in range(d):
                for k in range(nqb + 1):
                    Wt[(d, r, k)] = Wall[d][:, r * (nqb + 1) + k, :]
            r_Wt[d] = Res()
            wstate[d] = "ones"
        ACC2 = [alloc(nc, es, "sb", "ACC", [128, 2, SEG], F32) for _ in range(2)]
        tmp = alloc(nc, es, "sb", "tmpn", [128, SEG], F32)
        r_tmp = Res()
        brT = Rot([alloc(nc, es, "sb", "brT%d" % i, [128, SEG], BF16) for i in range(2)])
        Pt = Rot([alloc(nc, es, "sb", "Pt%d" % i, [128, 2, 256], BF16) for i in range(4)])
        SpE = Rot([alloc(nc, es, "ps", "SpE%d" % i, [128, 512], F32) for i in range(2)])
        SpO = Rot([alloc(nc, es, "ps", "SpO%d" % i, [128, 512], F32) for i in range(2)])
        PV = Rot([alloc(nc, es, "ps", "PV%d" % i, [128, 4, 128], F32) for i in range(3)])

        r_eb = Res()
        S.dma(DMA(EB[:, :, :], G.EB_d), "cst", writes=[r_eb])
        for d in DILS:
            S.op("pool", MEMSET(Wall[d][:, :, :], 1.0), writes=[r_Wt[d]])
        S.ready(["dve", "pool"], [r_eb])
        emic = [0]
        pend = []
        perm_todo = []
        r_pt2 = [[Res(), Res()] for _ in range(4)]
        r_acc2 = [{1: Res(), 4: Res(), 16: Res()} for _ in range(2)]
        r_nrm2 = [{"dve": Res(), "pool": Res()} for _ in range(2)]
        pending_norm = []

        NPC = 8
        r_tmp_pc = [Res() for _ in range(NPC)]

        def normalize_gen(ci):
            sidx_, j_ = chunks[ci]
            ACC_ = ACC2[ci % 2]
            ra = r_acc2[ci % 2][16]
            br_, r_br, bi = brT.next()
            W_ = SEG // NPC
            for pc in range(NPC):
                cs = slice(pc * W_, (pc + 1) * W_)
                last = (pc == NPC - 1)
                rt = r_tmp_pc[pc]
                recip_act(S, tmp[0:64, cs], ACC_[64:128, 0, cs], [ra], rt)
                recip_act(S, tmp[64:128, cs], ACC_[0:64, 1, cs], [ra, rt], rt,
                          weak=[r_nrm2[ci % 2]["dve"]] if last else ())
                kw = dict(writes=[r_br]) if pc == 0 else dict(weak=[r_br])
                S.op("pool", TT(br_[0:64, cs], ACC_[0:64, 0, cs], tmp[0:64, cs], ALU.mult), reads=[ra, rt], **kw)
                S.op("pool", TT(br_[64:128, cs], ACC_[64:128, 1, cs], tmp[64:128, cs], ALU.mult), reads=[ra, rt],
                     weak=[r_br] + ([r_nrm2[ci % 2]["pool"]] if last else []))
                if not last:
                    yield
            S.dma(DMA(G.BRB[j_, :, sidx_ * SEG:(sidx_ + 1) * SEG], br_[:, :]), ("brs", bi), reads=[r_br], writes=[Res()])

        def normalize(ci):
            for _ in normalize_gen(ci):
                pass

        norm_gen = [None]
        chunks = [(sidx, j) for sidx in range(nseg) for j in range(4)]
        groups = [(sidx, j, d) for (sidx, j) in chunks for d in DILS]
        qk = {}

        def load_qk(ci):
            sidx, j = chunks[ci]
            s0 = sidx * SEG
            qn, r_qn, qi = Qn.next()
            kn, r_kn, ki = Kn.next()
            S.dma(DMA(qn[:, :], G.QBT[j, :, s0:s0 + SEG]), ("qn", qi), writes=[r_qn])
            a, b = max(0, s0 - KH), min(NTOK, s0 + SEG + KH)
            if a > s0 - KH:
                S.op("pool", MEMSET(kn[:, 0:KH], 0.0), writes=[r_kn])
            if b < s0 + SEG + KH:
                S.op("pool", MEMSET(kn[:, KH + SEG:KH + SEG + KH], 0.0), writes=[r_kn])
            S.dma(DMA(kn[:, a - (s0 - KH):b - (s0 - KH)], G.KBT[j, :, a:b]), ("kn", ki), writes=[r_kn])
            qk[ci] = (qn, r_qn, kn, r_kn)

        def permute_pieces(ci):
            qn, r_qn, kn, r_kn = qk[ci]
            out = []
            for dd in (4, 16):
                qsrc = qn[:, :].rearrange("p (m r) -> p r m", r=dd)
                ksrc = kn[:, KH - 64 * dd:KH + SEG + 64 * dd].rearrange("p (m r) -> p r m", r=dd)
                qdst, kdst = Qd2[ci % 2][dd], Kd2[ci % 2][dd]
                rq, rk = r_Qd2[ci % 2][dd], r_Kd2[ci % 2][dd]
                step = dd // 4
                for r0 in range(0, dd, step):
                    def fq(qdst=qdst, qsrc=qsrc, r0=r0, step=step, rq=rq, first=(r0 == 0)):
                        kw = dict(writes=[rq]) if first else dict(weak=[rq])
                        S.op("pool", COPY("pool", qdst[:, r0:r0 + step, :], qsrc[:, r0:r0 + step, :]), reads=[r_qn], **kw)

                    def fk(kdst=kdst, ksrc=ksrc, r0=r0, step=step, rk=rk, first=(r0 == 0)):
                        kw = dict(writes=[rk]) if first else dict(weak=[rk])
                        S.op("act", COPY("act", kdst[:, r0:r0 + step, :], ksrc[:, r0:r0 + step, :]), reads=[r_kn], **kw)
                    out.append(fq)
                    out.append(fk)
            return out

        def permute_qk(ci):
            for f in permute_pieces(ci):
                f()

        def load_w(gi):
            sidx, j, d = groups[gi]
            s0 = sidx * SEG
            nqb = SEG // (128 * d)
            nk = nqb + 1
            W4 = Wall[d][:, :, :].rearrange("p (r k) c -> p r k c", r=d)
            r_w = r_Wt[d]
            lo_edge = (sidx == 0)
            hi_edge = (sidx == nseg - 1)
            want = ("lo" if lo_edge else "") + ("hi" if hi_edge else "") or "ones"
            if wstate[d] != want:
                if wstate[d] != "ones":
                    S.op("pool", MEMSET(W4[:, :, :, 0:64], 1.0), writes=[r_w])
                    S.op("pool", MEMSET(W4[:, :, :, 192:256], 1.0), writes=[r_w])
                if lo_edge:
                    S.op("pool", MEMSET(W4[0:64, :, 0, :], 0.0), writes=[r_w])
                if hi_edge:
                    S.op("pool", MEMSET(W4[64:128, :, nqb, :], 0.0), writes=[r_w])
                wstate[d] = want
            c0 = 128 + 128 * j
            B = s0 - 64 * d
            klo = 1 if lo_edge else 0
            khi = nqb - 1 if hi_edge else nqb

            Vr = G.V.rearrange("(m r) c -> r m c", r=d)
            m0 = B // d

            def vdma(e):
                out = []
                for r in range(d):
                    if khi >= klo:
                        src = Vr[r, m0 + 128 * klo:m0 + 128 * (khi + 1), c0:c0 + 128].rearrange("(k p) c -> p k c", p=128)
                        out.append(e.dma_start(out=W4[:, r, klo:khi + 1, 64:192], in_=src))
                    if lo_edge:
                        out.append(e.dma_start(out=W4[64:128, r, 0, 64:192], in_=Vr[r, m0 + 64:m0 + 128, c0:c0 + 128]))
                    if hi_edge:
                        m2 = m0 + 128 * nqb
                        out.append(e.dma_start(out=W4[0:64, r, nqb, 64:192], in_=Vr[r, m2:m2 + 64, c0:c0 + 128]))
                return out
            nd = d * ((1 if khi >= klo else 0) + (1 if lo_edge else 0) + (1 if hi_edge else 0))
            S.dma(vdma, "wt%d" % d, writes=[r_w], ndma=nd)
            return [r_w]

        load_qk(0)
        permute_qk(0)
        wres = {0: load_w(0)} if DBG >= 2 else {}
        for gi, (sidx, j, d) in enumerate(groups):
            ci = sidx * 4 + j
            s0 = sidx * SEG
            qn, r_qn, kn, r_kn = qk[ci]
            Qd, Kd, r_Qd, r_Kd = Qd2[ci % 2], Kd2[ci % 2], r_Qd2[ci % 2], r_Kd2[ci % 2]
            if d == 1 and ci + 1 < len(chunks):
                load_qk(ci + 1)
            if d == 4 and ci + 1 < len(chunks):
                perm_todo.extend(permute_pieces(ci + 1))
            if DBG < 2:
                continue
            if gi + 1 < len(groups):
                wres[gi + 1] = load_w(gi + 1)
            S.ready(["pe"], wres.pop(gi))
            first_cfg = (d == 1)
            nsub = SEG // d
            nqb = nsub // 128
            lg = {1: 0, 4: 2, 16: 4}[d]
            xi_e = 2 * j + 1 - lg + 3
            xi_o = 2 * j + 2 - lg + 3
            ACC = ACC2[ci % 2]
            r_acc = r_acc2[ci % 2]
            r_nrm = r_nrm2[ci % 2]
            ACCv = ACC[:, :, :].rearrange("p e (m r) -> p e r m", r=d)
            prev_acc = [r_nrm["dve"], r_nrm["pool"]] if first_cfg else [r_acc[{4: 1, 16: 4}[d]]]
            def group_fns(d, nqb, nsub, xi_e, xi_o, ACCv, first_cfg, prev_acc, r_acc_d, qn, r_qn, kn, r_kn, Qd, Kd, r_Qd, r_Kd):
                slot = {}
                def emit_score(r, k):
                    qlo, qhi = max(0, 128 * (k - 1)), min(nsub, 128 * (k + 1))
                    N = qhi - qlo
                    if d == 1:
                        kE = kn[0:64, KH - 64 + 128 * k:KH + 64 + 128 * k]
                        kO = kn[64:128, KH - 64 + 128 * k:KH + 64 + 128 * k]
                        qE = qn[0:64, qlo:qhi]
                        qO = qn[64:128, qlo:qhi]
                        rd = [r_kn, r_qn]
                    else:
                        kE = Kd[d][0:64, r, 128 * k:128 * k + 128]
                        kO = Kd[d][64:128, r, 128 * k:128 * k + 128]
                        qE = Qd[d][0:64, r, qlo:qhi]
                        qO = Qd[d][64:128, r, qlo:qhi]
                        rd = [r_Kd[d], r_Qd[d]]
                    se_, r_se, _ = SpE.next()
                    so_, r_so, _ = SpO.next()
                    S.op("pe", MM([(se_[:, 0:N], kE, qE, True, True), (so_[:, 0:N], kO, qO, True, True)]),
                         reads=rd, writes=[r_se, r_so])
                    p_, _, pi = Pt.next()
                    rp = r_pt2[pi]
                    S.op("act", ACTF(p_[:, 0, 0:N], se_[:, 0:N], AF.Exp, scale=0.125), reads=[r_se], writes=[rp[0]])
                    S.op("act", ACTF(p_[:, 1, 0:N], so_[:, 0:N], AF.Exp, scale=0.125), reads=[r_so], writes=[rp[1]])
                    tc0 = 256 if k == 0 else (384 if k == nqb else 0)
                    e1 = "dve" if emic[0] % 2 == 0 else "pool"
                    e2 = "pool" if emic[0] % 2 == 0 else "dve"
                    if emic[0] % 3 == 2:
                        e1 = e2 = "dve"
                    emic[0] += 1
                    S.op(e1, TT(p_[:, 0, 0:N], p_[:, 0, 0:N], EB[:, xi_e, tc0:tc0 + N], ALU.mult), reads=[rp[0]], writes=[rp[0]])
                    S.op(e2, TT(p_[:, 1, 0:N], p_[:, 1, 0:N], EB[:, xi_o, tc0:tc0 + N], ALU.mult), reads=[rp[1]], writes=[rp[1]])
                    return (r, k, p_, rp, qlo)

                def emit_pv(job):
                    r, k, p_, rp, qlo = job
                    w = Wt[(d, r, k)]
                    r_w = r_Wt[d]
                    mms = []
                    wr = []
                    for bq in (k - 1, k):
                        if bq < 0 or bq >= nqb:
                            continue
                        if (r, bq) not in slot:
                            slot[(r, bq)] = PV.next()
                        pv_, r_pv, _ = slot[(r, bq)]
                        off = bq * 128 - qlo
                        mms.append((pv_[:, 0, :], w[:, 128:256], p_[:, 0, off:off + 128], bq == k, bq == k - 1))
                        mms.append((pv_[:, 1, :], w[:, 0:128], p_[:, 1, off:off + 128], False, False, True))
                        wr.append(r_pv)
                    S.op("pe", MM(mms), reads=[rp[0], rp[1], r_w], writes=wr)
                    if k >= 1:
                        pv_, r_pv, _ = slot[(r, k - 1)]
                        dst = ACCv[:, :, r, 128 * (k - 1):128 * k]
                        if first_cfg:
                            S.op("dve", COPY("dve", dst, pv_[:, 0:2, :]), reads=[r_pv] + prev_acc, weak=[r_acc_d])
                        else:
                            S.op("dve", TT(dst, pv_[:, 0:2, :], dst, ALU.add), reads=[r_pv] + prev_acc, weak=[r_acc_d])

                return emit_score, emit_pv

            emit_score, emit_pv = group_fns(d, nqb, nsub, xi_e, xi_o, ACCv, first_cfg, prev_acc, r_acc[d], qn, r_qn, kn, r_kn,
                                            Qd, Kd, r_Qd, r_Kd)
            LOOK = 1
            ntile = 0
            for r in range(d):
                for k in range(nqb + 1):
                    pend.append((emit_pv, emit_score(r, k)))
                    if len(pend) > LOOK:
                        fn_, job_ = pend.pop(0)
                        fn_(job_)
                    ntile += 1
                    if perm_todo:
                        perm_todo.pop(0)()
                    if d == 1 and ntile >= 4 and (pending_norm or norm_gen[0] is not None):
                        if norm_gen[0] is None:
                            norm_gen[0] = normalize_gen(pending_norm.pop(0))
                        try:
                            next(norm_gen[0])
                        except StopIteration:
                            norm_gen[0] = None
            if d == 1 and norm_gen[0] is not None:
                for _ in norm_gen[0]:
                    pass
                norm_gen[0] = None
            if d == 16:
                while perm_todo:
                    perm_todo.pop(0)()
                if gi + 1 == len(groups):
                    while pend:
                        fn_, job_ = pend.pop(0)
                        fn_(job_)
                pending_norm.append(ci)
        while pending_norm:
            normalize(pending_norm.pop(0))
        S.barrier_and_emit()


def stage_memkv(nc, S, G, w_mkv):
    nseg = G.nseg
    with ExitStack() as es:
        C = Ctx()
        Wm = alloc(nc, es, "sb", "Wm", [128, 8, 1024], BF16)
        C.ident = alloc(nc, es, "sb", "ident", [128, 128], BF16)
        C.xb = Rot([alloc(nc, es, "sb", "xb%d" % i, [128, 1024], BF16) for i in range(2)])
        mT = Rot([alloc(nc, es, "sb", "mT%d" % i, [128, 8, 256], BF16) for i in range(2)])
        r_mT = [[Res(), Res()] for _ in range(2)]
        kst = Rot([alloc(nc, es, "sb", "kst%d" % i, [128, 4, 256], BF16) for i in range(2)])
        vst = Rot([alloc(nc, es, "sb", "vst%d" % i, [128, 512], BF16) for i in range(2)])
        C.pT = Rot([alloc(nc, es, "ps", "pT%d" % i, [128, 8, 128], BF16) for i in range(1)])
        pk = Rot([alloc(nc, es, "ps", "pk%d" % i, [128, 512], F32) for i in range(2)])
        pv = Rot([alloc(nc, es, "ps", "pv%d" % i, [128, 512], F32) for i in range(2)])
        r_c = Res()
        S.dma(DMA(C.ident[:, :], G.ident_d), "cst", writes=[r_c])
        r_w = load_weights_cast(S, [(Wm[:, kc, :], w_mkv[kc * 128:(kc + 1) * 128, :]) for kc in range(8)])
        S.ready(["pe"], r_w + [r_c])
        for s in range(nseg):
            mT_, _, mi = mT.next()
            load_x_transposed(S, C, lambda mt: G.mem[s, mt * 128:(mt + 1) * 128, :], mT_, r_mT[mi], 2)
            k_, r_k, ki = kst.next()
            for h in range(4):
                p_, r_p, _ = pk.next()
                S.op("pe", MM([(p_[:, 0:256], Wm[:, kc, h * 128:(h + 1) * 128], mT_[:, kc, :], kc == 0, kc == 7)
                               for kc in range(8)]), reads=r_mT[mi], writes=[r_p])
                S.op("dve", COPY("dve", k_[:, h, :], p_[:, 0:256]), reads=[r_p], writes=[r_k])
            S.dma(DMA(G.KMT[s], k_[:, :, :]), ("kst", ki), reads=[r_k], writes=[Res()])
            for mt in range(2):
                p_, r_p, _ = pv.next()
                S.op("pe", MM([(p_[:, :], mT_[:, kc, mt * 128:(mt + 1) * 128], Wm[:, kc, 512:1024], kc == 0, kc == 7)
                               for kc in range(8)]), reads=r_mT[mi], writes=[r_p])
                v_, r_v, vi = vst.next()
                S.op("act", COPY("act", v_[:, :], p_[:, :]), reads=[r_p], writes=[r_v])
                S.dma(DMA(G.VM[s, mt * 128:(mt + 1) * 128, :], v_[:, :]), ("vst", vi), reads=[r_v], writes=[Res()])
        S.barrier_and_emit()


def stage_mix(nc, S, G, x_in, x_out, sink, w_br, w_o, g_row, b_row, T3=256):
    NTOK, nseg = G.NTOK, G.nseg
    NT = NTOK // T3
    NKT = NTOK // 128
    with ExitStack() as es:
        C = Ctx()
        Wbr = alloc(nc, es, "sb", "Wbr", [128, 3, 4, 1024], BF16)
        Wo = alloc(nc, es, "sb", "Wo", [128, 8, 1024], BF16)
        EA = alloc(nc, es, "sb", "EA", [128, 10, 512], BF16)
        ones = alloc(nc, es, "sb", "ones", [128, 128], BF16)
        sel = alloc(nc, es, "sb", "sel", [1, 256], F32)
        sk = alloc(nc, es, "sb", "sk", [1, 16], F32)
        onesrow = alloc(nc, es, "sb", "onesrow", [1, 128], F32)
        esrow = alloc(nc, es, "sb", "esrow", [1, 2, 2, 2, 128], F32)
        alloc_end_phase(nc, es, C, 4)
        Qa = Rot([alloc(nc, es, "sb", "Qa%d" % i, [128, 4, T3], BF16) for i in range(2)])
        KaW = Rot([alloc(nc, es, "sb", "KaW%d" % i, [128, 2, 512], BF16) for i in range(2)])
        WAt = Rot([alloc(nc, es, "sb", "WAt%d" % i, [128, 4, 2, 192], BF16) for i in range(2)])
        Qm = Rot([alloc(nc, es, "sb", "Qm%d" % i, [128, 4, T3], BF16) for i in range(2)])
        BRt = Rot([alloc(nc, es, "sb", "BRt%d" % i, [128, 4, T3], BF16) for i in range(2)])
        GTt = Rot([alloc(nc, es, "sb", "GTt%d" % i, [128, 3, 8, T3], BF16) for i in range(2)])
        kmT = Rot([alloc(nc, es, "sb", "kmT%d" % i, [128, 4, 256], BF16) for i in range(2)])
        vm = Rot([alloc(nc, es, "sb", "vm%d" % i, [128, 2, 512], BF16) for i in range(2)])
        Pt = Rot([alloc(nc, es, "sb", "Pt%d" % i, [128, 512], BF16) for i in range(3)])
        brA = alloc(nc, es, "sb", "brA", [128, 4, T3], BF16)
        r_brA = [[Res(), Res()] for _ in range(4)]
        brM = alloc(nc, es, "sb", "brM", [128, 4, T3], BF16)
        r_brM = [Res() for _ in range(4)]
        tmpA = Rot([alloc(nc, es, "sb", "tmpA%d" % i, [128, 4, 128], F32) for i in range(2)])
        tmpM = Rot([alloc(nc, es, "sb", "tmpM%d" % i, [128, T3], F32) for i in range(2)])
        tg = Rot([alloc(nc, es, "sb", "tg%d" % i, [128, 3, T3], F32) for i in range(2)])
        mixT = alloc(nc, es, "sb", "mixT", [128, 8, T3], BF16)
        r_mix = [Res() for _ in range(8)]
        SpL = Rot([alloc(nc, es, "ps", "SpL%d" % i, [128, 512], F32) for i in range(2)])
        SpU = Rot([alloc(nc, es, "ps", "SpU%d" % i, [128, 512], F32) for i in range(2)])
        PVb = Rot([alloc(nc, es, "ps", "PVb%d" % i, [128, 512], F32) for i in range(2)])
        pjx = alloc(nc, es, "ps", "pjx", [128, 512], F32)
        r_pjx = Res()
        pj_sets = [(PVb.tiles[0], PVb.res[0], PVb.tiles[1], PVb.res[1]), (pjx, r_pjx, SpU.tiles[1], SpU.res[1])]
        r_c = [Res() for _ in range(4)]
        S.dma(DMA(EA[:, :, :], G.EA_d), "cst", writes=[r_c[0]])
        S.dma(DMA(sel[:, :], G.sel_d), "cst4", writes=[r_c[1]])
        S.dma(DMA(sk[0:1, 0:8], sink.rearrange("(o h) -> o h", o=1)), "cst5", writes=[r_c[2]])
        r_c.append(load_bcast_row(S, nc, C.gt[:, :], g_row, "cst2"))
        r_c.append(load_bcast_row(S, nc, C.bt[:, :], b_row, "cst3"))
        S.op("pool", MEMSET(ones[:, :], 1.0), writes=[r_c[3]])
        r_or = Res()
        S.op("pool", MEMSET(onesrow[:, :], 1.0), writes=[r_or])
        r_sk = Res()
        S.op("act", ACTF(sk[0:1, 8:16], sk[0:1, 0:8], AF.Exp), reads=[r_c[2]], writes=[r_sk])
        r_es = Res()
        for g in range(2):
            for half in range(2):
                for c in range(2):
                    h = 2 * (2 * g + c) + half
                    S.op("dve", TS(esrow[0:1, g, half, c, :], onesrow[0:1, :], sk[0:1, 8 + h:9 + h], None, ALU.mult),
                         reads=[r_sk, r_or], writes=[r_es])
        pieces = []
        for i in range(3):
            for kc in range(4):
                pieces.append((Wbr[:, i, kc, :], w_br[i, kc * 128:(kc + 1) * 128, :]))
        for kc in range(8):
            pieces.append((Wo[:, kc, :], w_o[kc * 128:(kc + 1) * 128, :]))
        r_w = load_weights_cast(S, pieces)
        for t_, r_ in zip(WAt.tiles, WAt.res):
            S.op("pool", MEMSET(t_[:, :, :, :], 1.0), writes=[r_])
        S.ready(["pe"], r_w + r_c + [r_es])
        S.ready(["dve", "pool", "act"], r_c)

        emi = 0
        segbuf = {}

        def load_tile(t):
            t0 = t * T3
            kb = t0 // 128
            sidx = t0 // SEG
            if sidx not in segbuf:
                km_, r_km, ki = kmT.next()
                vm_, r_vm, vi = vm.next()
                S.dma(DMA(km_[:, :, :], G.KMT[sidx]), ("km", ki), writes=[r_km])
                S.dma(DMA(vm_[:, :, :], G.VM[sidx].rearrange("(m p) c -> p m c", p=128)), ("vm", vi), writes=[r_vm])
                segbuf[sidx] = (km_, r_km, vm_, r_vm)
            qa, r_qa, qi = Qa.next()
            S.dma(DMA(qa[:, :, :], G.QAT[:, :, t0:t0 + T3].rearrange("c p t -> p c t")), ("qa", qi), writes=[r_qa])
            kw, r_kw, kwi = KaW.next()
            a, b = max(0, t0 - 128), min(NTOK, t0 + 384)
            S.dma(DMA(kw[:, :, a - (t0 - 128):b - (t0 - 128)], G.KAT[:, :, a:b].rearrange("c p t -> p c t")),
                  ("kw", kwi), writes=[r_kw])
            wa, r_wa, wi = WAt.next()
            kts = [kb - 1 + i for i in range(4)]
            vl = [i for i in range(4) if 0 <= kts[i] < NKT]

            def wadma(e, wa=wa, kts=kts, vl=vl):
                out = []
                for i in vl:
                    src = G.V[kts[i] * 128:(kts[i] + 1) * 128, 0:128].rearrange("p (g d) -> p g d", g=2)
                    out.append(e.dma_start(out=wa[:, i, :, 64:128], in_=src))
                return out
            S.dma(wadma, ("wa", wi), writes=[r_wa], ndma=len(vl))
            qm, r_qm, qmi = Qm.next()
            S.dma(DMA(qm[:, :, :], G.QMT[:, :, t0:t0 + T3].rearrange("c p t -> p c t")), ("qm", qmi), writes=[r_qm])
            brb, r_brb, bbi = BRt.next()
            S.dma(DMA(brb[:, :, :], G.BRB[:, :, t0:t0 + T3].rearrange("c p t -> p c t")), ("brb", bbi), writes=[r_brb])
            gt_, r_gt, gi = GTt.next()
            S.dma(DMA(gt_[:, :, :, :], G.GT[:, :, t0:t0 + T3].rearrange("(i o) p t -> p i o t", i=3)), ("gt", gi),
                  writes=[r_gt])
            rs = [prefetch_res(S, C, x_in[t0 + s * 128:t0 + (s + 1) * 128, :]) for s in range(T3 // 128)]
            return (qa, r_qa, kw, r_kw, wa, r_wa, qm, r_qm, brb, r_brb, gt_, r_gt, rs) + segbuf[sidx]

        nxt_tile = load_tile(0)
        pending_end = []
        cur_gen = [None]
        for t in range(NT):
            t0 = t * T3
            kb = t0 // 128
            sidx = t0 // SEG
            (qa, r_qa, kw, r_kw, wa, r_wa, qm, r_qm, brb, r_brb, gt_, r_gt, rslots, km_, r_km, vm_, r_vm) = nxt_tile

            def a_score(job):
                nonlocal emi
                qb, g, n_, kt, nk, pvslot = job
                qabs = kb + qb
                i = kt - (kb - 1)
                off = kt - qabs
                same = (kt * 128) // SEG == (qabs * 128) // SEG
                ty = off + 1 if same else (3 if off == -1 else 4)
                sl_, r_sl, _ = SpL.next()
                su_, r_su, _ = SpU.next()
                sl2 = sl_[:, 0:256].rearrange("p (a q) -> p a q", a=2)
                su2 = su_[:, 0:256].rearrange("p (a q) -> p a q", a=2)
                qs = slice(qb * 128, (qb + 1) * 128)
                S.op("pe", MM([(sl2, kw[0:64, g, i * 128:(i + 1) * 128], qa[0:64, 2 * g:2 * g + 2, qs], True, True),
                               (su2, kw[64:128, g, i * 128:(i + 1) * 128], qa[64:128, 2 * g:2 * g + 2, qs], True, True)]),
                     reads=[r_kw, r_qa], writes=[r_sl, r_su])
                p_, r_p, _ = Pt.next()
                S.op("act", ACTF(p_[:, 0:256], sl_[:, 0:256], AF.Exp, scale=0.125), reads=[r_sl], writes=[r_p])
                S.op("act", ACTF(p_[:, 256:512], su_[:, 0:256], AF.Exp, scale=0.125), reads=[r_su], writes=[r_p])
                e1 = "dve" if emi % 2 == 0 else "pool"
                emi += 1
                S.op(e1, TT(p_[:, :], p_[:, :], EA[:, g * 5 + ty, :], ALU.mult), reads=[r_p], writes=[r_p])
                return (job, p_, r_p, i)

            def a_pv(st):
                job, p_, r_p, i = st
                qb, g, n_, kt, nk, pvslot = job
                pv_, r_pv, _ = pvslot
                pv4 = pv_[:, :].rearrange("p (a q) -> p a q", a=4)
                p4 = p_[:, :].rearrange("p (a q) -> p a q", a=4)
                qs = slice(qb * 128, (qb + 1) * 128)
                S.op("pe", MM([(pv4[:, 0:2, :], wa[:, i, g, 64:192], p4[:, 0:2, :], n_ == 0, False),
                               (pv4[:, 2:4, :], wa[:, i, g, 0:128], p4[:, 2:4, :], False, False, True)]),
                     reads=[r_p, r_wa], writes=[r_pv])
                if n_ == nk - 1:
                    S.op("pe", MM([(pv4[:, 0:2, :], sel[0:1, 0:128], esrow[0:1, g, 0, :, :], False, True),
                                   (pv4[:, 2:4, :], sel[0:1, 128:256], esrow[0:1, g, 1, :, :], False, False, True)]),
                         writes=[r_pv])

                    def fin(pv4=pv4, r_pv=r_pv, qb=qb, g=g, qs=qs):
                        ta, r_ta, _ = tmpA.next()
                        rl, ru = r_brA[qb * 2 + g]
                        recip_act(S, ta[0:64, 0:2, :], pv4[64:128, 0:2, :], [r_pv], r_ta)
                        recip_act(S, ta[64:128, 2:4, :], pv4[0:64, 2:4, :], [r_pv, r_ta], r_ta)
                        S.op("dve", TT(brA[0:64, 2 * g:2 * g + 2, qs], pv4[0:64, 0:2, :], ta[0:64, 0:2, :], ALU.mult),
                             reads=[r_pv, r_ta], writes=[rl])
                        S.op("dve", TT(brA[64:128, 2 * g:2 * g + 2, qs], pv4[64:128, 2:4, :], ta[64:128, 2:4, :], ALU.mult),
                             reads=[r_pv, r_ta], writes=[ru])
                    deferred.append([2, fin])

            jobs = []
            for qb in range(T3 // 128):
                qabs = kb + qb
                for g in range(2):
                    klist = [kt for kt in (qabs - 1, qabs, qabs + 1) if 0 <= kt < NKT]
                    pvslot = PVb.next()
                    for n_, kt in enumerate(klist):
                        jobs.append((qb, g, n_, kt, len(klist), pvslot))

            def m_score(h):
                sp_, r_sp, _ = SpL.next()
                sp2 = sp_[:, :].rearrange("p (a q) -> p a q", a=2)
                S.op("pe", MM([(sp2[:, mt, :], km_[:, h, mt * 128:(mt + 1) * 128], qm[:, h, :], True, True) for mt in range(2)]),
                     reads=[r_km, r_qm], writes=[r_sp])
                p_, r_p, _ = Pt.next()
                S.op("act", ACTF(p_[:, :], sp_[:, :], AF.Exp, scale=128.0 ** -0.5), reads=[r_sp], writes=[r_p])
                return (h, p_, r_p)

            def m_pv(st):
                h, p_, r_p = st
                p2 = p_[:, :].rearrange("p (a q) -> p a q", a=2)
                pv_, r_pv = pjx, r_pjx
                pv2 = pv_[:, :].rearrange("p (a q) -> p a q", a=2)
                S.op("pe", MM([(pv2[:, 0, :], vm_[:, mt, h * 128:(h + 1) * 128], p2[:, mt, :], mt == 0, mt == 1) for mt in range(2)]
                              + [(pv2[:, 1, :], ones[:, :], p2[:, mt, :], mt == 0, mt == 1) for mt in range(2)]),
                     reads=[r_p, r_vm], writes=[r_pv])
                tm, r_tm, _ = tmpM.next()
                recip_act(S, tm[:, :], pv2[:, 1, :], [r_pv], r_tm)
                S.op("dve", TT(brM[:, h, :], pv2[:, 0, :], tm[:, :], ALU.mult), reads=[r_pv, r_tm], writes=[r_brM[h]])

            seq = []
            mi = 0
            for n_, job in enumerate(jobs):
                seq.append(("a", job))
                if n_ % 3 == 1 and mi < 4:
                    seq.append(("m", mi))
                    mi += 1
            while mi < 4:
                seq.append(("m", mi))
                mi += 1
            pend = []
            deferred = []
            for si_, (kind, job) in enumerate(seq):
                pend.append((kind, a_score(job) if kind == "a" else m_score(job)))
                if len(pend) > 1:
                    kd, st_ = pend.pop(0)
                    (a_pv if kd == "a" else m_pv)(st_)
                for dfr in list(deferred):
                    dfr[0] -= 1
                    if dfr[0] < 0:
                        deferred.remove(dfr)
                        dfr[1]()
                if si_ >= 3 and (pending_end or cur_gen[0] is not None):
                    if cur_gen[0] is None:
                        cur_gen[0] = end_phase_b_gen(S, C, *pending_end.pop(0))
                    try:
                        next(cur_gen[0])
                    except StopIteration:
                        cur_gen[0] = None
            while pend:
                kd, st_ = pend.pop(0)
                (a_pv if kd == "a" else m_pv)(st_)
            for dfr in deferred:
                dfr[1]()
            while cur_gen[0] is not None or pending_end:
                if cur_gen[0] is None:
                    cur_gen[0] = end_phase_b_gen(S, C, *pending_end.pop(0))
                for _ in cur_gen[0]:
                    pass
                cur_gen[0] = None
            if t + 1 < NT:
                nxt_tile = load_tile(t + 1)
            r_brAall = [x for pr in r_brA[:2 * (T3 // 128)] for x in pr]
            for oc in range(8):
                pja, r_pja, pjb, r_pjb = pj_sets[oc % 2]
                pj0 = pja[:, :].rearrange("p (a q) -> p a q", a=2)
                pj1 = pjb
                r_pjs = [r_pja, r_pjb]
                for bi_, brt, rds, wr_ in ((1, brb, [r_brb], r_pjs[0]), (2, brM, r_brM, r_pjs[1]), (0, brA, r_brAall, r_pjs[0])):
                    o_ = pj0[:, bi_, :] if bi_ < 2 else pj1[:, 0:256]
                    S.op("pe", MM([(o_, Wbr[:, bi_, kc, oc * 128:(oc + 1) * 128], brt[:, kc, :], kc == 0, kc == 3)
                                   for kc in range(4)]), reads=rds, writes=[wr_])
                tg_, r_tg, _ = tg.next()
                S.op("dve", TT(tg_[:, 0:2, :], pj0[:, 0:2, :], gt_[:, 0:2, oc, :], ALU.mult), reads=[r_pjs[0], r_gt], writes=[r_tg])
                S.op("dve", TT(tg_[:, 2, :], pj1[:, 0:256], gt_[:, 2, oc, :], ALU.mult), reads=[r_pjs[1], r_gt], writes=[r_tg])
                S.op("pool", TT(tg_[:, 0, :], tg_[:, 0, :], tg_[:, 1, :], ALU.add), reads=[r_tg], writes=[r_tg])
                S.op("pool", TT(mixT[:, oc, :], tg_[:, 0, :], tg_[:, 2, :], ALU.add), reads=[r_tg], writes=[r_mix[oc]])
            for kpass in range(2):
                for s in range(T3 // 128):
                    yparts = [SpL.tiles[s % 2], SpU.tiles[s % 2]]
                    ryh = [SpL.res[s % 2], SpU.res[s % 2]]
                    for hh in range(2):
                        S.op("pe", MM([(yparts[hh][:, :], mixT[:, kc, s * 128:(s + 1) * 128],
                                        Wo[:, kc, hh * 512:(hh + 1) * 512], kc == 0, kc == 7)
                                       for kc in range(4 * kpass, 4 * kpass + 4)]),
                             reads=r_mix[4 * kpass:4 * kpass + 4], writes=[ryh[hh]])
            for s in range(T3 // 128):
                yparts = [SpL.tiles[s % 2], SpU.tiles[s % 2]]
                ryh = [SpL.res[s % 2], SpU.res[s % 2]]
                rows = slice(t0 + s * 128, t0 + (s + 1) * 128)
                end_phase_a(S, C, [yparts[0][:, :], yparts[1][:, :]], ryh, None, 1.0 / ALPHA, rslots[s])
                pending_end.append((x_out[rows, :], rslots[s]))
        while pending_end:
            end_phase_b(S, C, *pending_end.pop(0))
        S.barrier_and_emit()


_CACHE = {}


def kernel(x_prompt, x_sample, mem_prompt, mem_sample, ffn1_w_in, ffn1_w_out, ln1_g, ln1_b, w_in, w_mem_kv,
           sink_a, w_branch, w_out, ln2_g, ln2_b, ffn2_w_in, ffn2_w_out, ln3_g, ln3_b):
    f32 = lambda a: np.ascontiguousarray(np.asarray(a), dtype=np.float32)
    x_prompt, x_sample, mem_prompt, mem_sample = f32(x_prompt), f32(x_sample), f32(mem_prompt), f32(mem_sample)
    wts = dict(ffn1_w_in=f32(ffn1_w_in), ffn1_w_out=f32(ffn1_w_out), ln1_g=f32(ln1_g), ln1_b=f32(ln1_b),
               w_in=f32(w_in), w_mem_kv=f32(w_mem_kv), sink_a=f32(sink_a), w_branch=f32(w_branch), w_out=f32(w_out),
               ln2_g=f32(ln2_g), ln2_b=f32(ln2_b), ffn2_w_in=f32(ffn2_w_in), ffn2_w_out=f32(ffn2_w_out),
               ln3_g=f32(ln3_g), ln3_b=f32(ln3_b))
    if "nc" not in _CACHE:
        _CACHE["nc"] = build(nseg=4, depth=2)[0]
    nc = _CACHE["nc"]
    tabs = [host_tables(0), host_tables(1)]
    in_maps = []
    for c in range(8):
        if c < 4:
            xc = x_prompt[4 * c:4 * c + 4].reshape(4 * SEG, D)
            mc = mem_prompt[4 * c:4 * c + 4]
            J = 0
        else:
            xc = x_sample[c - 4]
            mc = np.ascontiguousarray(np.broadcast_to(mem_sample[c - 4][None], (4, 256, D)))
            J = 1
        m = {"x": xc, "mem": mc}
        m.update(wts)
        m.update(tabs[J])
        in_maps.append(m)
    res = run_bass_kernel_spmd(nc, in_maps, core_ids=list(range(8)))
    y_prompt = np.stack([res.results[c]["y"] for c in range(4)]).reshape(16, SEG, D)
    y_sample = np.stack([res.results[c]["y"] for c in range(4, 8)]).reshape(4, 4 * SEG, D)
    return (y_prompt.astype(np.float32), y_sample.astype(np.float32))
```

```python
from contextlib import ExitStack

import numpy as np
import ml_dtypes
import concourse.bass as bass
import concourse.mybir as mybir
from concourse.bass_utils import run_bass_kernel_spmd

F32 = mybir.dt.float32
BF16 = mybir.dt.bfloat16
AF = mybir.ActivationFunctionType
ALU = mybir.AluOpType

SEG = 2048
D = 1024
DFF = 2816
ALPHA = 4.0 ** 0.25
LN_EPS = 1e-5
EPS2 = LN_EPS / (ALPHA * ALPHA)
ENGS = ("pe", "act", "dve", "pool", "sp")
NSEM_POOL = 96


class Res:
    __slots__ = ("lw", "rd")

    def __init__(self):
        self.lw = None
        self.rd = []


class Op:
    __slots__ = ("eng", "fn", "deps", "sig", "event", "dmakey", "ndma", "done")


class Sched:
    def __init__(self, nc, es):
        self.nc = nc
        self.pending = {e: [] for e in ENGS}
        self.allp = []
        self.cnt = {}
        self.waited = {e: {} for e in ENGS}
        self.sempool = [es.enter_context(nc.semaphore("sm%d" % i)) for i in range(NSEM_POOL)]
        self.semmap = {}
        self.nops = 0

    def sem(self, k):
        if k not in self.semmap:
            assert len(self.semmap) < NSEM_POOL, "out of semaphores"
            self.semmap[k] = self.sempool[len(self.semmap)]
        return self.semmap[k]

    def op(self, eng, fn, reads=(), writes=(), dmakey=None, ndma=1, weak=()):
        o = Op()
        o.eng, o.fn, o.dmakey, o.ndma = eng, fn, dmakey, ndma
        o.sig = False
        o.event = None
        o.done = False
        deps = set()
        for r in reads:
            if r.lw is not None and not r.lw.done:
                deps.add(r.lw)
        for w in writes:
            live = [x for x in w.rd if not x.done]
            if w.lw is not None and not w.lw.done and not live:
                deps.add(w.lw)
            deps.update(live)
        deps.discard(o)
        if fn is not None:
            for r in reads:
                r.rd.append(o)
        for w in writes:
            w.lw = o
            w.rd = []
        for w in weak:
            w.lw = o
        o.deps = deps
        self.pending[eng].append(o)
        self.allp.append(o)
        return o

    def dma(self, fn, key, reads=(), writes=(), ndma=1, eng="sp"):
        return self.op(eng, fn, reads, writes, dmakey=key, ndma=ndma)

    def ready(self, engs, res_list):
        for e in engs:
            self.op(e, None, reads=res_list)

    @staticmethod
    def _skip(o, d):
        return d.dmakey is None and o.dmakey is None and d.eng == "pe" and o.eng == "pe"

    def barrier_and_emit(self):
        lasts = []
        for e in ENGS:
            comp = [o for o in self.pending[e] if o.dmakey is None and o.fn is not None]
            if comp:
                lasts.append(comp[-1])
        lastkey = {}
        for o in self.allp:
            if o.dmakey is not None:
                lastkey[o.dmakey] = o
        lasts += list(lastkey.values())
        for e in ENGS:
            b = self.op(e, None)
            b.deps = set(lasts)
        self._emit()

    def _emit(self):
        nc = self.nc
        skip = self._skip
        for o in self.allp:
            for d in o.deps:
                if d.dmakey is None and not skip(o, d):
                    d.sig = True
        for o in self.allp:
            if o.dmakey is not None:
                k = ("dma", o.dmakey)
                self.cnt[k] = self.cnt.get(k, 0) + 16 * o.ndma
                o.event = (k, self.cnt[k])
            elif o.sig:
                k = ("eng", o.eng)
                self.cnt[k] = self.cnt.get(k, 0) + 1
                o.event = (k, self.cnt[k])
        with nc.Block() as block:
            def run(engname, e):
                waited = self.waited[engname]
                for o in self.pending[engname]:
                    need = {}
                    for d in o.deps:
                        if skip(o, d):
                            continue
                        k, v = d.event
                        if need.get(k, 0) < v:
                            need[k] = v
                    for k, v in need.items():
                        if waited.get(k, 0) < v:
                            e.wait_ge(self.sem(k), v)
                            waited[k] = v
                    if o.fn is None:
                        continue
                    r = o.fn(e)
                    self.nops += 1
                    if o.dmakey is not None:
                        if not isinstance(r, (list, tuple)):
                            r = [r]
                        assert len(r) == o.ndma
                        for ins in r:
                            ins.then_inc(self.sem(o.event[0]), 16)
                    elif o.sig:
                        if isinstance(r, (list, tuple)):
                            r = r[-1]
                        r.then_inc(self.sem(o.event[0]), 1)

            if self.pending["pe"]:
                @block.tensor
                def _(e):
                    run("pe", e)
            if self.pending["act"]:
                @block.scalar
                def _(e):
                    run("act", e)
            if self.pending["dve"]:
                @block.vector
                def _(e):
                    run("dve", e)
            if self.pending["pool"]:
                @block.gpsimd
                def _(e):
                    run("pool", e)
            if self.pending["sp"]:
                @block.sync
                def _(e):
                    run("sp", e)
        for o in self.allp:
            o.done = True
            o.deps = None
            o.fn = None
        self.pending = {e: [] for e in ENGS}
        self.allp = []


class Rot:
    def __init__(self, tiles):
        self.tiles = tiles
        self.res = [Res() for _ in tiles]
        self.i = 0

    def next(self):
        i = self.i % len(self.tiles)
        self.i += 1
        return self.tiles[i], self.res[i], i


def MM(lst):
    def fn(e):
        r = None
        for t in lst:
            o, l, rh, st, sp = t[:5]
            if len(t) > 5 and t[5]:
                r = e.matmul(o, lhsT=l, rhs=rh, start=st, stop=sp, skip_group_check=True)
            else:
                r = e.matmul(o, lhsT=l, rhs=rh, start=st, stop=sp)
        return r
    return fn


def TR(lst):
    def fn(e):
        r = None
        for (o, i, ident) in lst:
            r = e.transpose(out=o, in_=i, identity=ident)
        return r
    return fn


def ACTF(out, in_, func, scale=1.0):
    return lambda e: e.activation(out=out, in_=in_, func=func, scale=scale)


def COPY(eng, out, in_):
    if eng == "act":
        return lambda e: e.copy(out=out, in_=in_)
    return lambda e: e.tensor_copy(out=out, in_=in_)


def TT(out, in0, in1, op):
    return lambda e: e.tensor_tensor(out=out, in0=in0, in1=in1, op=op)


def TS(out, in0, s1, s2, op0, op1=None):
    if op1 is None:
        return lambda e: e.tensor_scalar(out=out, in0=in0, scalar1=s1, scalar2=None, op0=op0)
    return lambda e: e.tensor_scalar(out=out, in0=in0, scalar1=s1, scalar2=s2, op0=op0, op1=op1)


def STT(out, in0, scalar, in1, op0, op1):
    return lambda e: e.scalar_tensor_tensor(out=out, in0=in0, scalar=scalar, in1=in1, op0=op0, op1=op1)


def MEMSET(ap, v):
    return lambda e: e.memset(ap, v)


def RECIP(out, in_):
    return lambda e: e.reciprocal(out=out, in_=in_)


def recip_act(S, out, in_, reads, res, weak=()):
    S.op("act", ACTF(out, in_, AF.Ln), reads=reads, writes=[res])
    S.op("act", ACTF(out, out, AF.Exp, scale=-1.0), reads=[res], writes=[res], weak=weak)


def DMA(out, in_):
    return lambda e: e.dma_start(out=out, in_=in_)


class Ctx:
    pass


_uid = [0]


def alloc(nc, es, kind, name, shape, dt):
    _uid[0] += 1
    name = "%s_%d" % (name, _uid[0])
    if kind == "sb":
        return es.enter_context(nc.sbuf_tensor(name, shape, dt))
    return es.enter_context(nc.psum_tensor(name, shape, dt))


def load_weights_cast(S, pieces, key="w"):
    rl = []
    for dst, src in pieces:
        r = Res()
        S.dma(DMA(dst, src), key, writes=[r], eng="pool")
        rl.append(r)
    return rl


def load_bcast_row(S, nc, dst, src_row, key):
    r = Res()
    S.dma(DMA(dst, src_row.partition_broadcast(128)), key, writes=[r])
    return r


def end_phase_a(S, C, yparts, r_yh, res_src, coef, slot):
    xs_, r_xs, si = slot
    if res_src is not None:
        S.dma(DMA(xs_[:, :], res_src), ("xsl", si), writes=[r_xs])
    for h in range(2):
        S.op("dve", STT(xs_[:, h * 512:(h + 1) * 512], yparts[h], coef,
                        xs_[:, h * 512:(h + 1) * 512], ALU.mult, ALU.add),
             reads=[r_yh[h]], writes=[r_xs])


def end_phase_b(S, C, out_dst, slot):
    xs_, r_xs, si = slot
    mv = C.mv[si]
    st = C.st[si]
    r_mv = C.r_mv[si]
    S.op("dve", lambda e: e.bn_stats(out=st[:, 0:6], in_=xs_[:, 0:512]), reads=[r_xs], writes=[r_mv])
    S.op("dve", lambda e: e.bn_stats(out=st[:, 6:12], in_=xs_[:, 512:1024]), reads=[r_xs], writes=[r_mv])
    S.op("dve", lambda e: e.bn_aggr(out=mv[:, 0:2], in_=st[:, 0:12]), reads=[r_mv], writes=[r_mv])
    S.op("dve", TS(mv[:, 2:3], mv[:, 1:2], EPS2, None, ALU.add), reads=[r_mv], writes=[r_mv])
    if not getattr(C, "mh_init", False):
        S.op("pool", MEMSET(C.mhalf[:, :], -0.5), writes=[C.r_mh])
        C.mh_init = True
    S.op("pool", TT(mv[:, 4:5], mv[:, 2:3], C.mhalf[:, 0:1], ALU.pow), reads=[r_mv, C.r_mh], writes=[r_mv])
    S.op("dve", TS(xs_[:, :], xs_[:, :], mv[:, 0:1], mv[:, 4:5], ALU.subtract, ALU.mult),
         reads=[r_mv, r_xs], writes=[r_xs])
    S.op("pool", TT(xs_[:, :], xs_[:, :], C.gt[:, :], ALU.mult), reads=[r_xs], writes=[r_xs])
    S.op("pool", TT(xs_[:, :], xs_[:, :], C.bt[:, :], ALU.add), reads=[r_xs], writes=[r_xs])
    S.dma(DMA(out_dst, xs_[:, :]), ("xss", si), reads=[r_xs], writes=[Res()])


def end_phase_b_gen(S, C, out_dst, slot):
    xs_, r_xs, si = slot
    mv = C.mv[si]
    st = C.st[si]
    r_mv = C.r_mv[si]
    S.op("dve", lambda e: e.bn_stats(out=st[:, 0:6], in_=xs_[:, 0:512]), reads=[r_xs], writes=[r_mv])
    yield
    S.op("dve", lambda e: e.bn_stats(out=st[:, 6:12], in_=xs_[:, 512:1024]), reads=[r_xs], writes=[r_mv])
    yield
    S.op("dve", lambda e: e.bn_aggr(out=mv[:, 0:2], in_=st[:, 0:12]), reads=[r_mv], writes=[r_mv])
    S.op("dve", TS(mv[:, 2:3], mv[:, 1:2], EPS2, None, ALU.add), reads=[r_mv], writes=[r_mv])
    yield
    S.op("act", ACTF(mv[:, 3:4], mv[:, 2:3], AF.Ln), reads=[r_mv], writes=[r_mv])
    S.op("act", ACTF(mv[:, 4:5], mv[:, 3:4], AF.Exp, scale=-0.5), reads=[r_mv], writes=[r_mv])
    yield
    S.op("dve", TS(xs_[:, :], xs_[:, :], mv[:, 0:1], mv[:, 4:5], ALU.subtract, ALU.mult),
         reads=[r_mv, r_xs], writes=[r_xs])
    yield
    S.op("pool", TT(xs_[:, :], xs_[:, :], C.gt[:, :], ALU.mult), reads=[r_xs], writes=[r_xs])
    yield
    S.op("pool", TT(xs_[:, :], xs_[:, :], C.bt[:, :], ALU.add), reads=[r_xs], writes=[r_xs])
    yield
    S.dma(DMA(out_dst, xs_[:, :]), ("xss", si), reads=[r_xs], writes=[Res()])


def end_phase(S, C, yparts, r_yh, res_src, out_dst, coef, slot):
    end_phase_a(S, C, yparts, r_yh, res_src, coef, slot)
    end_phase_b(S, C, out_dst, slot)


def prefetch_res(S, C, src):
    slot = C.xs.next()
    xs_, r_xs, si = slot
    S.dma(DMA(xs_[:, :], src), ("xsl", si), writes=[r_xs])
    return slot


def alloc_end_phase(nc, es, C, nslots=4):
    C.xs = Rot([alloc(nc, es, "sb", "xs%d" % i, [128, 1024], F32) for i in range(nslots)])
    C.mv = [alloc(nc, es, "sb", "mv%d" % i, [128, 8], F32) for i in range(nslots)]
    C.st = [alloc(nc, es, "sb", "st%d" % i, [128, 12], F32) for i in range(nslots)]
    C.r_mv = [Res() for _ in range(nslots)]
    C.gt = alloc(nc, es, "sb", "gt", [128, 1024], F32)
    C.bt = alloc(nc, es, "sb", "bt", [128, 1024], F32)
    C.mhalf = alloc(nc, es, "sb", "mhalf", [128, 2], F32)
    C.r_mh = Res()


def load_x_transposed(S, C, x_src_rows, xT_dst, r_dst, nsub):
    for s in range(nsub):
        xb, r_xb, bi = C.xb.next()
        S.dma(DMA(xb[:, :], x_src_rows(s)), ("xb", bi), writes=[r_xb], eng="pool")
        pt, r_pt, _ = C.pT.next()
        S.op("pe", TR([(pt[:, c, :], xb[:, c * 128:(c + 1) * 128], C.ident[:, :]) for c in range(8)]),
             reads=[r_xb], writes=[r_pt])
        S.op("act", COPY("act", xT_dst[:, :, s * 128:(s + 1) * 128], pt[:, :, :]), reads=[r_pt], writes=[r_dst[s]])


def stage_ffn(nc, S, G, x_in, x_out, w_in, w_out, g_row, b_row, TF=256):
    NT = G.NTOK // TF
    NS = TF // 128
    with ExitStack() as es:
        C = Ctx()
        Win = alloc(nc, es, "sb", "Win", [128, 8, 2 * DFF], BF16)
        Wout = alloc(nc, es, "sb", "Wout", [128, 22, 1024], BF16)
        C.ident = alloc(nc, es, "sb", "ident", [128, 128], BF16)
        alloc_end_phase(nc, es, C, 4)
        C.xb = Rot([alloc(nc, es, "sb", "xb%d" % i, [128, 1024], BF16) for i in range(4)])
        xT = Rot([alloc(nc, es, "sb", "xT%d" % i, [128, 8, TF], BF16) for i in range(2)])
        r_xTs = [[Res() for _ in range(NS)] for _ in range(2)]
        hT = alloc(nc, es, "sb", "hT", [128, 22, TF], BF16)
        r_hT = [Res() for _ in range(22)]
        sg = Rot([alloc(nc, es, "sb", "sg%d" % i, [128, TF], F32) for i in range(3)])
        C.pT = Rot([alloc(nc, es, "ps", "pT%d" % i, [128, 8, 128], BF16) for i in range(1)])
        gpb = Rot([alloc(nc, es, "ps", "gp%d" % i, [128, 512], F32) for i in range(2)])
        upb = Rot([alloc(nc, es, "ps", "up%d" % i, [128, 512], F32) for i in range(2)])
        yps = Rot([alloc(nc, es, "ps", "yp%d" % i, [128, 1024], F32) for i in range(1)])
        r_yh = [[Res(), Res()] for _ in range(len(yps.tiles))]

        r_c = [Res()]
        S.dma(DMA(C.ident[:, :], G.ident_d), "cst", writes=[r_c[0]])
        r_c.append(load_bcast_row(S, nc, C.gt[:, :], g_row, "cst2"))
        r_c.append(load_bcast_row(S, nc, C.bt[:, :], b_row, "cst3"))
        S.ready(["pe"], r_c)
        S.ready(["pool", "dve", "act"], r_c)

        def prep(t):
            t0_ = t * TF
            xTt_, _, xi_ = xT.next()
            load_x_transposed(S, C, lambda s: x_in[t0_ + s * 128:t0_ + (s + 1) * 128, :], xTt_, r_xTs[xi_], NS)
            return xTt_, xi_

        nxt = prep(0)
        WAVES = [(0, 6), (6, 12), (12, 22)]
        r_wave = []
        for wi_, (c0_, c1_) in enumerate(WAVES):
            pieces = []
            for kc in range(8):
                rows = slice(kc * 128, (kc + 1) * 128)
                pieces.append((Win[:, kc, c0_ * 128:c1_ * 128], w_in[rows, c0_ * 128:c1_ * 128]))
                pieces.append((Win[:, kc, DFF + c0_ * 128:DFF + c1_ * 128], w_in[rows, DFF + c0_ * 128:DFF + c1_ * 128]))
            r_wave.append(load_weights_cast(S, pieces, key="w%d" % wi_))
        r_wout = load_weights_cast(S, [(Wout[:, ch, :], w_out[ch * 128:(ch + 1) * 128, :]) for ch in range(22)], key="w3")
        wave_joined = [False] * 4
        for t in range(NT):
            t0 = t * TF
            xTt, xi = nxt
            rslots = [prefetch_res(S, C, x_in[t0 + s * 128:t0 + (s + 1) * 128, :]) for s in range(NS)]
            for ch in range(22):
                for wi_, (c0_, c1_) in enumerate(WAVES):
                    if ch == c0_ and not wave_joined[wi_]:
                        S.ready(["pe"], r_wave[wi_])
                        wave_joined[wi_] = True
                gt_, r_g, _ = gpb.next()
                ut_, r_u, _ = upb.next()
                g_ = gt_[:, 0:TF]
                u_ = ut_[:, 0:TF]
                S.op("pe", MM([(g_, Win[:, kc, ch * 128:(ch + 1) * 128], xTt[:, kc, :], kc == 0, kc == 7)
                               for kc in range(8)]), reads=r_xTs[xi], writes=[r_g])
                S.op("pe", MM([(u_, Win[:, kc, DFF + ch * 128:DFF + (ch + 1) * 128], xTt[:, kc, :], kc == 0, kc == 7)
                               for kc in range(8)]), reads=r_xTs[xi], writes=[r_u])
                sg_, r_sg, _ = sg.next()
                S.op("act", ACTF(sg_[:, :], g_, AF.Silu), reads=[r_g], writes=[r_sg])
                S.op("dve", TT(hT[:, ch, :], u_, sg_[:, :], ALU.mult), reads=[r_u, r_sg], writes=[r_hT[ch]])
                if ch == 10 and t + 1 < NT:
                    nxt = prep(t + 1)
            if not wave_joined[3]:
                S.ready(["pe"], r_wout)
                wave_joined[3] = True
            for s in range(NS):
                yp, _, yi = yps.next()
                for (c0_, c1_) in ((0, 11), (11, 22)):
                    for h in range(2):
                        S.op("pe", MM([(yp[:, h * 512:(h + 1) * 512], hT[:, ch, s * 128:(s + 1) * 128],
                                        Wout[:, ch, h * 512:(h + 1) * 512], ch == 0, ch == 21) for ch in range(c0_, c1_)]),
                             reads=r_hT[c0_:c1_], writes=[r_yh[yi][h]])
                rows = slice(t0 + s * 128, t0 + (s + 1) * 128)
                end_phase(S, C, [yp[:, 0:512], yp[:, 512:1024]], r_yh[yi], None, x_out[rows, :], 0.5 / ALPHA, rslots[s])
        S.barrier_and_emit()


def stage_proj(nc, S, G, x_in, w_in, T2=512):
    NT = G.NTOK // T2
    NS = T2 // 128
    NFM = 42
    with ExitStack() as es:
        C = Ctx()
        Wf = alloc(nc, es, "sb", "Wf", [128, 8, NFM * 128], BF16)
        Wv = alloc(nc, es, "sb", "Wv", [128, 8, 640], BF16)
        C.ident = alloc(nc, es, "sb", "ident", [128, 128], BF16)
        C.xb = Rot([alloc(nc, es, "sb", "xb%d" % i, [128, 1024], BF16) for i in range(4)])
        xT = Rot([alloc(nc, es, "sb", "xT%d" % i, [128, 8, T2], BF16) for i in range(2)])
        r_xTs = [[Res() for _ in range(NS)] for _ in range(2)]
        ost = Rot([alloc(nc, es, "sb", "ost%d" % i, [128, T2], BF16) for i in range(6)])
        vst = Rot([alloc(nc, es, "sb", "vst%d" % i, [128, 640], BF16) for i in range(3)])
        C.pT = Rot([alloc(nc, es, "ps", "pT%d" % i, [128, 8, 128], BF16) for i in range(1)])
        pf = Rot([alloc(nc, es, "ps", "pf%d" % i, [128, T2], F32) for i in range(3)])
        pv = Rot([alloc(nc, es, "ps", "pv%d" % i, [128, 512], F32) for i in range(2)])
        pv2 = Rot([alloc(nc, es, "ps", "pw%d" % i, [128, 512], F32) for i in range(1)])

        r_c = [Res()]
        S.dma(DMA(C.ident[:, :], G.ident_d), "cst", writes=[r_c[0]])
        S.ready(["pe"], r_c)
        first_x = xT.next()
        load_x_transposed(S, C, lambda s: x_in[s * 128:(s + 1) * 128, :], first_x[0], r_xTs[first_x[2]], NS)
        wv = [[], [], [], []]
        for kc in range(8):
            rows = slice(kc * 128, (kc + 1) * 128)
            wv[0].append((Wf[:, kc, 0:512], w_in[rows, 0:512]))
            for g in range(2):
                for dup in range(2):
                    c0 = (4 + g) * 128 + dup * 64
                    wv[0].append((Wf[:, kc, c0:c0 + 64], w_in[rows, 512 + 64 * g:512 + 64 * g + 64]))
            wv[1].append((Wf[:, kc, 6 * 128:14 * 128], w_in[rows, 768:1792]))
            wv[1].append((Wf[:, kc, 14 * 128:18 * 128], w_in[rows, 2304:2816]))
            wv[2].append((Wf[:, kc, 18 * 128:30 * 128], w_in[rows, 2816:2816 + 1536]))
            wv[2].append((Wf[:, kc, 30 * 128:42 * 128], w_in[rows, 2816 + 1536:5888]))
            pieces = wv[3]
            pieces.append((Wv[:, kc, 0:128], w_in[rows, 640:768]))
            for hh in range(8):
                dc = 128 + 128 * (hh // 2) + (0 if hh % 2 == 1 else 64)
                pieces.append((Wv[:, kc, dc:dc + 64], w_in[rows, 1792 + 64 * hh:1792 + 64 * hh + 64]))
        r_wave = [load_weights_cast(S, wv[i], key="w%d" % i) for i in range(4)]
        S.ready(["pe"], r_c)
        wave_joined = [False] * 4
        wave_start = {0: 0, 6: 1, 18: 2}

        def dst_of(oc):
            if oc < 4:
                return G.QAT, oc
            if oc < 6:
                return G.KAT, oc - 4
            if oc < 10:
                return G.QBT, oc - 6
            if oc < 14:
                return G.KBT, oc - 10
            if oc < 18:
                return G.QMT, oc - 14
            return G.GT, oc - 18

        evi = 0
        for t in range(NT):
            t0 = t * T2
            if t == 0:
                xTt, _, xi = first_x
            else:
                xTt, _, xi = xT.next()
                load_x_transposed(S, C, lambda s: x_in[t0 + s * 128:t0 + (s + 1) * 128, :], xTt, r_xTs[xi], NS)
            for oc in range(NFM):
                if oc in wave_start and not wave_joined[wave_start[oc]]:
                    S.ready(["pe"], r_wave[wave_start[oc]])
                    wave_joined[wave_start[oc]] = True
                p_, r_p, _ = pf.next()
                S.op("pe", MM([(p_[:, :], Wf[:, kc, oc * 128:(oc + 1) * 128], xTt[:, kc, :], kc == 0, kc == 7)
                               for kc in range(8)]), reads=r_xTs[xi], writes=[r_p])
                o_, r_o, oi = ost.next()
                if oc >= 18:
                    S.op("act", ACTF(o_[:, :], p_[:, :], AF.Sigmoid), reads=[r_p], writes=[r_o])
                else:
                    eng = "dve" if evi % 2 == 0 else "act"
                    evi += 1
                    S.op(eng, COPY(eng, o_[:, :], p_[:, :]), reads=[r_p], writes=[r_o])
                dt_, ci = dst_of(oc)
                S.dma(DMA(dt_[ci, :, t0:t0 + T2], o_[:, :]), ("ost", oi), reads=[r_o], writes=[Res()])
            if not wave_joined[3]:
                S.ready(["pe"], r_wave[3])
                wave_joined[3] = True
            for s in range(NS):
                a_, r_a, _ = pv.next()
                b_, r_b, _ = pv2.next()
                S.op("pe", MM([(a_[:, :], xTt[:, kc, s * 128:(s + 1) * 128], Wv[:, kc, 128:640], kc == 0, kc == 7)
                               for kc in range(8)]), reads=r_xTs[xi], writes=[r_a])
                S.op("pe", MM([(b_[:, 0:128], xTt[:, kc, s * 128:(s + 1) * 128], Wv[:, kc, 0:128], kc == 0, kc == 7)
                               for kc in range(8)]), reads=r_xTs[xi], writes=[r_b])
                v_, r_v, vi = vst.next()
                S.op("dve", COPY("dve", v_[:, 128:640], a_[:, :]), reads=[r_a], writes=[r_v])
                S.op("dve", COPY("dve", v_[:, 0:128], b_[:, 0:128]), reads=[r_b], writes=[r_v])
                S.dma(DMA(G.V[t0 + s * 128:t0 + (s + 1) * 128, :], v_[:, :]), ("vst", vi), reads=[r_v], writes=[Res()])
        S.barrier_and_emit()


WNAMES = [("ffn1_w_in", [D, 2 * DFF]), ("ffn1_w_out", [DFF, D]), ("ln1_g", [D]), ("ln1_b", [D]),
          ("w_in", [D, 5888]), ("w_mem_kv", [D, 1024]), ("sink_a", [8]), ("w_branch", [3, 512, D]),
          ("w_out", [D, D]), ("ln2_g", [D]), ("ln2_b", [D]), ("ffn2_w_in", [D, 2 * DFF]),
          ("ffn2_w_out", [DFF, D]), ("ln3_g", [D]), ("ln3_b", [D])]


def build(nseg=4, depth=2, upto=99, TF=256):
    nc = bass.Bass("TRN2", target_bir_lowering=False)
    G = Ctx()
    G.nseg = nseg
    G.NTOK = NTOK = nseg * SEG
    x = nc.dram_tensor("x", [NTOK, D], F32, kind="ExternalInput").ap()
    G.mem = nc.dram_tensor("mem", [nseg, 256, D], F32, kind="ExternalInput").ap()
    W = {}
    for nm, shp in WNAMES:
        W[nm] = nc.dram_tensor(nm, [depth] + shp, F32, kind="ExternalInput").ap()
    G.ident_d = nc.dram_tensor("ident", [128, 128], BF16, kind="ExternalInput").ap()
    G.EA_d = nc.dram_tensor("etab_a", [128, 10, 512], BF16, kind="ExternalInput").ap()
    G.EB_d = nc.dram_tensor("etab_b", [128, 12, 512], BF16, kind="ExternalInput").ap()
    G.sel_d = nc.dram_tensor("sel", [1, 256], F32, kind="ExternalInput").ap()
    y = nc.dram_tensor("y", [NTOK, D], F32, kind="ExternalOutput").ap()
    XA = nc.dram_tensor("XA", [NTOK, D], F32).ap()
    XB = nc.dram_tensor("XB", [NTOK, D], F32).ap()
    XC = nc.dram_tensor("XC", [NTOK, D], F32).ap()
    G.QAT = nc.dram_tensor("QAT", [4, 128, NTOK], BF16).ap()
    G.KAT = nc.dram_tensor("KAT", [2, 128, NTOK], BF16).ap()
    G.QBT = nc.dram_tensor("QBT", [4, 128, NTOK], BF16).ap()
    G.KBT = nc.dram_tensor("KBT", [4, 128, NTOK], BF16).ap()
    G.QMT = nc.dram_tensor("QMT", [4, 128, NTOK], BF16).ap()
    G.GT = nc.dram_tensor("GT", [24, 128, NTOK], BF16).ap()
    G.V = nc.dram_tensor("V", [NTOK, 640], BF16).ap()
    G.BRB = nc.dram_tensor("BRB", [4, 128, NTOK], BF16).ap()
    G.KMT = nc.dram_tensor("KMT", [nseg, 128, 4, 256], BF16).ap()
    G.VM = nc.dram_tensor("VM", [nseg, 256, 512], BF16).ap()
    nst = 0
    with ExitStack() as es:
        S = Sched(nc, es)
        cur = x
        for l in range(depth):
            last = (l == depth - 1)
            nst += 1
            if nst > upto:
                break
            stage_ffn(nc, S, G, cur, XA if upto > nst else y, W["ffn1_w_in"][l], W["ffn1_w_out"][l],
                      W["ln1_g"][l], W["ln1_b"][l], TF)
            nst += 1
            if nst > upto:
                break
            stage_proj(nc, S, G, XA, W["w_in"][l])
            nst += 1
            if nst > upto:
                break
            stage_attn_b(nc, S, G)
            nst += 1
            if nst > upto:
                break
            stage_memkv(nc, S, G, W["w_mem_kv"][l])
            stage_mix(nc, S, G, XA, XB if upto > nst else y, W["sink_a"][l], W["w_branch"][l],
                      W["w_out"][l], W["ln2_g"][l], W["ln2_b"][l])
            nst += 1
            if nst > upto:
                break
            stage_ffn(nc, S, G, XB, y if (last or upto <= nst) else XC, W["ffn2_w_in"][l], W["ffn2_w_out"][l],
                      W["ln3_g"][l], W["ln3_b"][l], TF)
            cur = XC
        G.nops = S.nops
        G.nsem = len(S.semmap)
    return nc, G


def host_tables(J):
    bf = ml_dtypes.bfloat16
    j = np.arange(128)[:, None].astype(np.float64)
    i = np.arange(128)[None, :].astype(np.float64)
    ea = np.zeros((128, 10, 512), np.float64)
    for g in range(2):
        for ty in range(5):
            off = [-1, 0, 1, -1, 1][ty]
            rel = off * 128 + j - i
            valid = (np.abs(rel) <= 128)
            for half in range(2):
                for c in range(2):
                    h = 2 * (2 * g + c) + half
                    slope = 2.0 ** (-(h + 1))
                    e = np.exp(-slope * np.abs(rel)) * valid
                    if ty >= 3 and J == 0:
                        e = e * 0.0
                    col = half * 256 + c * 128
                    ea[:, g * 5 + ty, col:col + 128] = e
    eb = np.zeros((128, 12, 512), np.float64)
    for xi in range(12):
        se = 2.0 ** (-(xi - 3))
        rel_prev = j - i + 64
        rel_cur = j - 64 - i
        e_prev = np.exp(-se * np.abs(rel_prev)) * (np.abs(rel_prev) <= 64)
        e_cur = np.exp(-se * np.abs(rel_cur)) * (np.abs(rel_cur) <= 64)
        eb[:, xi, 0:128] = e_prev
        eb[:, xi, 128:256] = e_cur
        lo = e_cur.copy()
        hi = e_prev.copy()
        if J == 0:
            lo[0:64, :] = 0.0
            hi[64:128, :] = 0.0
        eb[:, xi, 256:384] = lo
        eb[:, xi, 384:512] = hi
    sel = np.zeros((1, 256), np.float32)
    sel[0, 64:128] = 1.0
    sel[0, 128:192] = 1.0
    return {"ident": np.eye(128).astype(bf), "etab_a": ea.astype(np.float32).astype(bf),
            "etab_b": eb.astype(np.float32).astype(bf), "sel": sel}


KH = 1024


def stage_attn_b(nc, S, G):
    import os
    DBG = int(os.environ.get('ATTNB_DBG', '9'))
    NTOK, nseg = G.NTOK, G.nseg
    DILS = (1, 4, 16)
    with ExitStack() as es:
        EB = alloc(nc, es, "sb", "EB", [128, 12, 512], BF16)
        Qn = Rot([alloc(nc, es, "sb", "Qn%d" % i, [128, SEG], BF16) for i in range(2)])
        Kn = Rot([alloc(nc, es, "sb", "Kn%d" % i, [128, SEG + 2 * KH], BF16) for i in range(2)])
        Qd2 = [{4: alloc(nc, es, "sb", "Qd4", [128, 4, 512], BF16), 16: alloc(nc, es, "sb", "Qd16", [128, 16, 128], BF16)} for _ in range(2)]
        Kd2 = [{4: alloc(nc, es, "sb", "Kd4", [128, 4, 640], BF16), 16: alloc(nc, es, "sb", "Kd16", [128, 16, 256], BF16)} for _ in range(2)]
        r_Qd2 = [{4: Res(), 16: Res()} for _ in range(2)]
        r_Kd2 = [{4: Res(), 16: Res()} for _ in range(2)]
        Wall, Wt, r_Wt, wstate = {}, {}, {}, {}
        for d in DILS:
            nqb = SEG // (128 * d)
            Wall[d] = alloc(nc, es, "sb", "Wall%d" % d, [128, d * (nqb + 1), 256], BF16)
            for r in range(d):
                for k in range(nqb + 1):
                    Wt[(d, r, k)] = Wall[d][:, r * (nqb + 1) + k, :]
            r_Wt[d] = Res()
            wstate[d] = "ones"
        ACC2 = [alloc(nc, es, "sb", "ACC", [128, 2, SEG], F32) for _ in range(2)]
        tmp = alloc(nc, es, "sb", "tmpn", [128, SEG], F32)
        r_tmp = Res()
        brT = Rot([alloc(nc, es, "sb", "brT%d" % i, [128, SEG], BF16) for i in range(2)])
        Pt = Rot([alloc(nc, es, "sb", "Pt%d" % i, [128, 2, 256], BF16) for i in range(4)])
        SpE = Rot([alloc(nc, es, "ps", "SpE%d" % i, [128, 512], F32) for i in range(2)])
        SpO = Rot([alloc(nc, es, "ps", "SpO%d" % i, [128, 512], F32) for i in range(2)])
        PV = Rot([alloc(nc, es, "ps", "PV%d" % i, [128, 4, 128], F32) for i in range(3)])

        r_eb = Res()
        S.dma(DMA(EB[:, :, :], G.EB_d), "cst", writes=[r_eb])
        for d in DILS:
            S.op("pool", MEMSET(Wall[d][:, :, :], 1.0), writes=[r_Wt[d]])
        S.ready(["dve", "pool"], [r_eb])
        emic = [0]
        pend = []
        perm_todo = []
        r_pt2 = [[Res(), Res()] for _ in range(4)]
        r_acc2 = [{1: Res(), 4: Res(), 16: Res()} for _ in range(2)]
        r_nrm2 = [{"dve": Res(), "pool": Res()} for _ in range(2)]
        pending_norm = []

        NPC = 8
        r_tmp_pc = [Res() for _ in range(NPC)]

        def normalize_gen(ci):
            sidx_, j_ = chunks[ci]
            ACC_ = ACC2[ci % 2]
            ra = r_acc2[ci % 2][16]
            br_, r_br, bi = brT.next()
            W_ = SEG // NPC
            for pc in range(NPC):
                cs = slice(pc * W_, (pc + 1) * W_)
                last = (pc == NPC - 1)
                rt = r_tmp_pc[pc]
                recip_act(S, tmp[0:64, cs], ACC_[64:128, 0, cs], [ra], rt)
                recip_act(S, tmp[64:128, cs], ACC_[0:64, 1, cs], [ra, rt], rt,
                          weak=[r_nrm2[ci % 2]["dve"]] if last else ())
                kw = dict(writes=[r_br]) if pc == 0 else dict(weak=[r_br])
                S.op("pool", TT(br_[0:64, cs], ACC_[0:64, 0, cs], tmp[0:64, cs], ALU.mult), reads=[ra, rt], **kw)
                S.op("pool", TT(br_[64:128, cs], ACC_[64:128, 1, cs], tmp[64:128, cs], ALU.mult), reads=[ra, rt],
                     weak=[r_br] + ([r_nrm2[ci % 2]["pool"]] if last else []))
                if not last:
                    yield
            S.dma(DMA(G.BRB[j_, :, sidx_ * SEG:(sidx_ + 1) * SEG], br_[:, :]), ("brs", bi), reads=[r_br], writes=[Res()])

        def normalize(ci):
            for _ in normalize_gen(ci):
                pass

        norm_gen = [None]
        chunks = [(sidx, j) for sidx in range(nseg) for j in range(4)]
        groups = [(sidx, j, d) for (sidx, j) in chunks for d in DILS]
        qk = {}

        def load_qk(ci):
            sidx, j = chunks[ci]
            s0 = sidx * SEG
            qn, r_qn, qi = Qn.next()
            kn, r_kn, ki = Kn.next()
            S.dma(DMA(qn[:, :], G.QBT[j, :, s0:s0 + SEG]), ("qn", qi), writes=[r_qn])
            a, b = max(0, s0 - KH), min(NTOK, s0 + SEG + KH)
            if a > s0 - KH:
                S.op("pool", MEMSET(kn[:, 0:KH], 0.0), writes=[r_kn])
            if b < s0 + SEG + KH:
                S.op("pool", MEMSET(kn[:, KH + SEG:KH + SEG + KH], 0.0), writes=[r_kn])
            S.dma(DMA(kn[:, a - (s0 - KH):b - (s0 - KH)], G.KBT[j, :, a:b]), ("kn", ki), writes=[r_kn])
            qk[ci] = (qn, r_qn, kn, r_kn)

        def permute_pieces(ci):
            qn, r_qn, kn, r_kn = qk[ci]
            out = []
            for dd in (4, 16):
                qsrc = qn[:, :].rearrange("p (m r) -> p r m", r=dd)
                ksrc = kn[:, KH - 64 * dd:KH + SEG + 64 * dd].rearrange("p (m r) -> p r m", r=dd)
                qdst, kdst = Qd2[ci % 2][dd], Kd2[ci % 2][dd]
                rq, rk = r_Qd2[ci % 2][dd], r_Kd2[ci % 2][dd]
                step = dd // 4
                for r0 in range(0, dd, step):
                    def fq(qdst=qdst, qsrc=qsrc, r0=r0, step=step, rq=rq, first=(r0 == 0)):
                        kw = dict(writes=[rq]) if first else dict(weak=[rq])
                        S.op("pool", COPY("pool", qdst[:, r0:r0 + step, :], qsrc[:, r0:r0 + step, :]), reads=[r_qn], **kw)

                    def fk(kdst=kdst, ksrc=ksrc, r0=r0, step=step, rk=rk, first=(r0 == 0)):
                        kw = dict(writes=[rk]) if first else dict(weak=[rk])
                        S.op("act", COPY("act", kdst[:, r0:r0 + step, :], ksrc[:, r0:r0 + step, :]), reads=[r_kn], **kw)
                    out.append(fq)
                    out.append(fk)
            return out

        def permute_qk(ci):
            for f in permute_pieces(ci):
                f()

        def load_w(gi):
            sidx, j, d = groups[gi]
            s0 = sidx * SEG
            nqb = SEG // (128 * d)
            nk = nqb + 1
            W4 = Wall[d][:, :, :].rearrange("p (r k) c -> p r k c", r=d)
            r_w = r_Wt[d]
            lo_edge = (sidx == 0)
            hi_edge = (sidx == nseg - 1)
            want = ("lo" if lo_edge else "") + ("hi" if hi_edge else "") or "ones"
            if wstate[d] != want:
                if wstate[d] != "ones":
                    S.op("pool", MEMSET(W4[:, :, :, 0:64], 1.0), writes=[r_w])
                    S.op("pool", MEMSET(W4[:, :, :, 192:256], 1.0), writes=[r_w])
                if lo_edge:
                    S.op("pool", MEMSET(W4[0:64, :, 0, :], 0.0), writes=[r_w])
                if hi_edge:
                    S.op("pool", MEMSET(W4[64:128, :, nqb, :], 0.0), writes=[r_w])
                wstate[d] = want
            c0 = 128 + 128 * j
            B = s0 - 64 * d
            klo = 1 if lo_edge else 0
            khi = nqb - 1 if hi_edge else nqb

            Vr = G.V.rearrange("(m r) c -> r m c", r=d)
            m0 = B // d

            def vdma(e):
                out = []
                for r in range(d):
                    if khi >= klo:
                        src = Vr[r, m0 + 128 * klo:m0 + 128 * (khi + 1), c0:c0 + 128].rearrange("(k p) c -> p k c", p=128)
                        out.append(e.dma_start(out=W4[:, r, klo:khi + 1, 64:192], in_=src))
                    if lo_edge:
                        out.append(e.dma_start(out=W4[64:128, r, 0, 64:192], in_=Vr[r, m0 + 64:m0 + 128, c0:c0 + 128]))
                    if hi_edge:
                        m2 = m0 + 128 * nqb
                        out.append(e.dma_start(out=W4[0:64, r, nqb, 64:192], in_=Vr[r, m2:m2 + 64, c0:c0 + 128]))
                return out
            nd = d * ((1 if khi >= klo else 0) + (1 if lo_edge else 0) + (1 if hi_edge else 0))
            S.dma(vdma, "wt%d" % d, writes=[r_w], ndma=nd)
            return [r_w]

        load_qk(0)
        permute_qk(0)
        wres = {0: load_w(0)} if DBG >= 2 else {}
        for gi, (sidx, j, d) in enumerate(groups):
            ci = sidx * 4 + j
            s0 = sidx * SEG
            qn, r_qn, kn, r_kn = qk[ci]
            Qd, Kd, r_Qd, r_Kd = Qd2[ci % 2], Kd2[ci % 2], r_Qd2[ci % 2], r_Kd2[ci % 2]
            if d == 1 and ci + 1 < len(chunks):
                load_qk(ci + 1)
            if d == 4 and ci + 1 < len(chunks):
                perm_todo.extend(permute_pieces(ci + 1))
            if DBG < 2:
                continue
            if gi + 1 < len(groups):
                wres[gi + 1] = load_w(gi + 1)
            S.ready(["pe"], wres.pop(gi))
            first_cfg = (d == 1)
            nsub = SEG // d
            nqb = nsub // 128
            lg = {1: 0, 4: 2, 16: 4}[d]
            xi_e = 2 * j + 1 - lg + 3
            xi_o = 2 * j + 2 - lg + 3
            ACC = ACC2[ci % 2]
            r_acc = r_acc2[ci % 2]
            r_nrm = r_nrm2[ci % 2]
            ACCv = ACC[:, :, :].rearrange("p e (m r) -> p e r m", r=d)
            prev_acc = [r_nrm["dve"], r_nrm["pool"]] if first_cfg else [r_acc[{4: 1, 16: 4}[d]]]
            def group_fns(d, nqb, nsub, xi_e, xi_o, ACCv, first_cfg, prev_acc, r_acc_d, qn, r_qn, kn, r_kn, Qd, Kd, r_Qd, r_Kd):
                slot = {}
                def emit_score(r, k):
                    qlo, qhi = max(0, 128 * (k - 1)), min(nsub, 128 * (k + 1))
                    N = qhi - qlo
                    if d == 1:
                        kE = kn[0:64, KH - 64 + 128 * k:KH + 64 + 128 * k]
                        kO = kn[64:128, KH - 64 + 128 * k:KH + 64 + 128 * k]
                        qE = qn[0:64, qlo:qhi]
                        qO = qn[64:128, qlo:qhi]
                        rd = [r_kn, r_qn]
                    else:
                        kE = Kd[d][0:64, r, 128 * k:128 * k + 128]
                        kO = Kd[d][64:128, r, 128 * k:128 * k + 128]
                        qE = Qd[d][0:64, r, qlo:qhi]
                        qO = Qd[d][64:128, r, qlo:qhi]
                        rd = [r_Kd[d], r_Qd[d]]
                    se_, r_se, _ = SpE.next()
                    so_, r_so, _ = SpO.next()
                    S.op("pe", MM([(se_[:, 0:N], kE, qE, True, True), (so_[:, 0:N], kO, qO, True, True)]),
                         reads=rd, writes=[r_se, r_so])
                    p_, _, pi = Pt.next()
                    rp = r_pt2[pi]
                    S.op("act", ACTF(p_[:, 0, 0:N], se_[:, 0:N], AF.Exp, scale=0.125), reads=[r_se], writes=[rp[0]])
                    S.op("act", ACTF(p_[:, 1, 0:N], so_[:, 0:N], AF.Exp, scale=0.125), reads=[r_so], writes=[rp[1]])
                    tc0 = 256 if k == 0 else (384 if k == nqb else 0)
                    e1 = "dve" if emic[0] % 2 == 0 else "pool"
                    e2 = "pool" if emic[0] % 2 == 0 else "dve"
                    if emic[0] % 3 == 2:
                        e1 = e2 = "dve"
                    emic[0] += 1
                    S.op(e1, TT(p_[:, 0, 0:N], p_[:, 0, 0:N], EB[:, xi_e, tc0:tc0 + N], ALU.mult), reads=[rp[0]], writes=[rp[0]])
                    S.op(e2, TT(p_[:, 1, 0:N], p_[:, 1, 0:N], EB[:, xi_o, tc0:tc0 + N], ALU.mult), reads=[rp[1]], writes=[rp[1]])
                    return (r, k, p_, rp, qlo)

                def emit_pv(job):
                    r, k, p_, rp, qlo = job
                    w = Wt[(d, r, k)]
                    r_w = r_Wt[d]
                    mms = []
                    wr = []
                    for bq in (k - 1, k):
                        if bq < 0 or bq >= nqb:
                            continue
                        if (r, bq) not in slot:
                            slot[(r, bq)] = PV.next()
                        pv_, r_pv, _ = slot[(r, bq)]
                        off = bq * 128 - qlo
                        mms.append((pv_[:, 0, :], w[:, 128:256], p_[:, 0, off:off + 128], bq == k, bq == k - 1))
                        mms.append((pv_[:, 1, :], w[:, 0:128], p_[:, 1, off:off + 128], False, False, True))
                        wr.append(r_pv)
                    S.op("pe", MM(mms), reads=[rp[0], rp[1], r_w], writes=wr)
                    if k >= 1:
                        pv_, r_pv, _ = slot[(r, k - 1)]
                        dst = ACCv[:, :, r, 128 * (k - 1):128 * k]
                        if first_cfg:
                            S.op("dve", COPY("dve", dst, pv_[:, 0:2, :]), reads=[r_pv] + prev_acc, weak=[r_acc_d])
                        else:
                            S.op("dve", TT(dst, pv_[:, 0:2, :], dst, ALU.add), reads=[r_pv] + prev_acc, weak=[r_acc_d])

                return emit_score, emit_pv

            emit_score, emit_pv = group_fns(d, nqb, nsub, xi_e, xi_o, ACCv, first_cfg, prev_acc, r_acc[d], qn, r_qn, kn, r_kn,
                                            Qd, Kd, r_Qd, r_Kd)
            LOOK = 1
            ntile = 0
            for r in range(d):
                for k in range(nqb + 1):
                    pend.append((emit_pv, emit_score(r, k)))
                    if len(pend) > LOOK:
                        fn_, job_ = pend.pop(0)
                        fn_(job_)
                    ntile += 1
                    if perm_todo:
                        perm_todo.pop(0)()
                    if d == 1 and ntile >= 4 and (pending_norm or norm_gen[0] is not None):
                        if norm_gen[0] is None:
                            norm_gen[0] = normalize_gen(pending_norm.pop(0))
                        try:
                            next(norm_gen[0])
                        except StopIteration:
                            norm_gen[0] = None
            if d == 1 and norm_gen[0] is not None:
                for _ in norm_gen[0]:
                    pass
                norm_gen[0] = None
            if d == 16:
                while perm_todo:
                    perm_todo.pop(0)()
                if gi + 1 == len(groups):
                    while pend:
                        fn_, job_ = pend.pop(0)
                        fn_(job_)
                pending_norm.append(ci)
        while pending_norm:
            normalize(pending_norm.pop(0))
        S.barrier_and_emit()


def stage_memkv(nc, S, G, w_mkv):
    nseg = G.nseg
    with ExitStack() as es:
        C = Ctx()
        Wm = alloc(nc, es, "sb", "Wm", [128, 8, 1024], BF16)
        C.ident = alloc(nc, es, "sb", "ident", [128, 128], BF16)
        C.xb = Rot([alloc(nc, es, "sb", "xb%d" % i, [128, 1024], BF16) for i in range(2)])
        mT = Rot([alloc(nc, es, "sb", "mT%d" % i, [128, 8, 256], BF16) for i in range(2)])
        r_mT = [[Res(), Res()] for _ in range(2)]
        kst = Rot([alloc(nc, es, "sb", "kst%d" % i, [128, 4, 256], BF16) for i in range(2)])
        vst = Rot([alloc(nc, es, "sb", "vst%d" % i, [128, 512], BF16) for i in range(2)])
        C.pT = Rot([alloc(nc, es, "ps", "pT%d" % i, [128, 8, 128], BF16) for i in range(1)])
        pk = Rot([alloc(nc, es, "ps", "pk%d" % i, [128, 512], F32) for i in range(2)])
        pv = Rot([alloc(nc, es, "ps", "pv%d" % i, [128, 512], F32) for i in range(2)])
        r_c = Res()
        S.dma(DMA(C.ident[:, :], G.ident_d), "cst", writes=[r_c])
        r_w = load_weights_cast(S, [(Wm[:, kc, :], w_mkv[kc * 128:(kc + 1) * 128, :]) for kc in range(8)])
        S.ready(["pe"], r_w + [r_c])
        for s in range(nseg):
            mT_, _, mi = mT.next()
            load_x_transposed(S, C, lambda mt: G.mem[s, mt * 128:(mt + 1) * 128, :], mT_, r_mT[mi], 2)
            k_, r_k, ki = kst.next()
            for h in range(4):
                p_, r_p, _ = pk.next()
                S.op("pe", MM([(p_[:, 0:256], Wm[:, kc, h * 128:(h + 1) * 128], mT_[:, kc, :], kc == 0, kc == 7)
                               for kc in range(8)]), reads=r_mT[mi], writes=[r_p])
                S.op("dve", COPY("dve", k_[:, h, :], p_[:, 0:256]), reads=[r_p], writes=[r_k])
            S.dma(DMA(G.KMT[s], k_[:, :, :]), ("kst", ki), reads=[r_k], writes=[Res()])
            for mt in range(2):
                p_, r_p, _ = pv.next()
                S.op("pe", MM([(p_[:, :], mT_[:, kc, mt * 128:(mt + 1) * 128], Wm[:, kc, 512:1024], kc == 0, kc == 7)
                               for kc in range(8)]), reads=r_mT[mi], writes=[r_p])
                v_, r_v, vi = vst.next()
                S.op("act", COPY("act", v_[:, :], p_[:, :]), reads=[r_p], writes=[r_v])
                S.dma(DMA(G.VM[s, mt * 128:(mt + 1) * 128, :], v_[:, :]), ("vst", vi), reads=[r_v], writes=[Res()])
        S.barrier_and_emit()


def stage_mix(nc, S, G, x_in, x_out, sink, w_br, w_o, g_row, b_row, T3=256):
    NTOK, nseg = G.NTOK, G.nseg
    NT = NTOK // T3
    NKT = NTOK // 128
    with ExitStack() as es:
        C = Ctx()
        Wbr = alloc(nc, es, "sb", "Wbr", [128, 3, 4, 1024], BF16)
        Wo = alloc(nc, es, "sb", "Wo", [128, 8, 1024], BF16)
        EA = alloc(nc, es, "sb", "EA", [128, 10, 512], BF16)
        ones = alloc(nc, es, "sb", "ones", [128, 128], BF16)
        sel = alloc(nc, es, "sb", "sel", [1, 256], F32)
        sk = alloc(nc, es, "sb", "sk", [1, 16], F32)
        onesrow = alloc(nc, es, "sb", "onesrow", [1, 128], F32)
        esrow = alloc(nc, es, "sb", "esrow", [1, 2, 2, 2, 128], F32)
        alloc_end_phase(nc, es, C, 4)
        Qa = Rot([alloc(nc, es, "sb", "Qa%d" % i, [128, 4, T3], BF16) for i in range(2)])
        KaW = Rot([alloc(nc, es, "sb", "KaW%d" % i, [128, 2, 512], BF16) for i in range(2)])
        WAt = Rot([alloc(nc, es, "sb", "WAt%d" % i, [128, 4, 2, 192], BF16) for i in range(2)])
        Qm = Rot([alloc(nc, es, "sb", "Qm%d" % i, [128, 4, T3], BF16) for i in range(2)])
        BRt = Rot([alloc(nc, es, "sb", "BRt%d" % i, [128, 4, T3], BF16) for i in range(2)])
        GTt = Rot([alloc(nc, es, "sb", "GTt%d" % i, [128, 3, 8, T3], BF16) for i in range(2)])
        kmT = Rot([alloc(nc, es, "sb", "kmT%d" % i, [128, 4, 256], BF16) for i in range(2)])
        vm = Rot([alloc(nc, es, "sb", "vm%d" % i, [128, 2, 512], BF16) for i in range(2)])
        Pt = Rot([alloc(nc, es, "sb", "Pt%d" % i, [128, 512], BF16) for i in range(3)])
        brA = alloc(nc, es, "sb", "brA", [128, 4, T3], BF16)
        r_brA = [[Res(), Res()] for _ in range(4)]
        brM = alloc(nc, es, "sb", "brM", [128, 4, T3], BF16)
        r_brM = [Res() for _ in range(4)]
        tmpA = Rot([alloc(nc, es, "sb", "tmpA%d" % i, [128, 4, 128], F32) for i in range(2)])
        tmpM = Rot([alloc(nc, es, "sb", "tmpM%d" % i, [128, T3], F32) for i in range(2)])
        tg = Rot([alloc(nc, es, "sb", "tg%d" % i, [128, 3, T3], F32) for i in range(2)])
        mixT = alloc(nc, es, "sb", "mixT", [128, 8, T3], BF16)
        r_mix = [Res() for _ in range(8)]
        SpL = Rot([alloc(nc, es, "ps", "SpL%d" % i, [128, 512], F32) for i in range(2)])
        SpU = Rot([alloc(nc, es, "ps", "SpU%d" % i, [128, 512], F32) for i in range(2)])
        PVb = Rot([alloc(nc, es, "ps", "PVb%d" % i, [128, 512], F32) for i in range(2)])
        pjx = alloc(nc, es, "ps", "pjx", [128, 512], F32)
        r_pjx = Res()
        pj_sets = [(PVb.tiles[0], PVb.res[0], PVb.tiles[1], PVb.res[1]), (pjx, r_pjx, SpU.tiles[1], SpU.res[1])]
        r_c = [Res() for _ in range(4)]
        S.dma(DMA(EA[:, :, :], G.EA_d), "cst", writes=[r_c[0]])
        S.dma(DMA(sel[:, :], G.sel_d), "cst4", writes=[r_c[1]])
        S.dma(DMA(sk[0:1, 0:8], sink.rearrange("(o h) -> o h", o=1)), "cst5", writes=[r_c[2]])
        r_c.append(load_bcast_row(S, nc, C.gt[:, :], g_row, "cst2"))
        r_c.append(load_bcast_row(S, nc, C.bt[:, :], b_row, "cst3"))
        S.op("pool", MEMSET(ones[:, :], 1.0), writes=[r_c[3]])
        r_or = Res()
        S.op("pool", MEMSET(onesrow[:, :], 1.0), writes=[r_or])
        r_sk = Res()
        S.op("act", ACTF(sk[0:1, 8:16], sk[0:1, 0:8], AF.Exp), reads=[r_c[2]], writes=[r_sk])
        r_es = Res()
        for g in range(2):
            for half in range(2):
                for c in range(2):
                    h = 2 * (2 * g + c) + half
                    S.op("dve", TS(esrow[0:1, g, half, c, :], onesrow[0:1, :], sk[0:1, 8 + h:9 + h], None, ALU.mult),
                         reads=[r_sk, r_or], writes=[r_es])
        pieces = []
        for i in range(3):
            for kc in range(4):
                pieces.append((Wbr[:, i, kc, :], w_br[i, kc * 128:(kc + 1) * 128, :]))
        for kc in range(8):
            pieces.append((Wo[:, kc, :], w_o[kc * 128:(kc + 1) * 128, :]))
        r_w = load_weights_cast(S, pieces)
        for t_, r_ in zip(WAt.tiles, WAt.res):
            S.op("pool", MEMSET(t_[:, :, :, :], 1.0), writes=[r_])
        S.ready(["pe"], r_w + r_c + [r_es])
        S.ready(["dve", "pool", "act"], r_c)

        emi = 0
        segbuf = {}

        def load_tile(t):
            t0 = t * T3
            kb = t0 // 128
            sidx = t0 // SEG
            if sidx not in segbuf:
                km_, r_km, ki = kmT.next()
                vm_, r_vm, vi = vm.next()
                S.dma(DMA(km_[:, :, :], G.KMT[sidx]), ("km", ki), writes=[r_km])
                S.dma(DMA(vm_[:, :, :], G.VM[sidx].rearrange("(m p) c -> p m c", p=128)), ("vm", vi), writes=[r_vm])
                segbuf[sidx] = (km_, r_km, vm_, r_vm)
            qa, r_qa, qi = Qa.next()
            S.dma(DMA(qa[:, :, :], G.QAT[:, :, t0:t0 + T3].rearrange("c p t -> p c t")), ("qa", qi), writes=[r_qa])
            kw, r_kw, kwi = KaW.next()
            a, b = max(0, t0 - 128), min(NTOK, t0 + 384)
            S.dma(DMA(kw[:, :, a - (t0 - 128):b - (t0 - 128)], G.KAT[:, :, a:b].rearrange("c p t -> p c t")),
                  ("kw", kwi), writes=[r_kw])
            wa, r_wa, wi = WAt.next()
            kts = [kb - 1 + i for i in range(4)]
            vl = [i for i in range(4) if 0 <= kts[i] < NKT]

            def wadma(e, wa=wa, kts=kts, vl=vl):
                out = []
                for i in vl:
                    src = G.V[kts[i] * 128:(kts[i] + 1) * 128, 0:128].rearrange("p (g d) -> p g d", g=2)
                    out.append(e.dma_start(out=wa[:, i, :, 64:128], in_=src))
                return out
            S.dma(wadma, ("wa", wi), writes=[r_wa], ndma=len(vl))
            qm, r_qm, qmi = Qm.next()
            S.dma(DMA(qm[:, :, :], G.QMT[:, :, t0:t0 + T3].rearrange("c p t -> p c t")), ("qm", qmi), writes=[r_qm])
            brb, r_brb, bbi = BRt.next()
            S.dma(DMA(brb[:, :, :], G.BRB[:, :, t0:t0 + T3].rearrange("c p t -> p c t")), ("brb", bbi), writes=[r_brb])
            gt_, r_gt, gi = GTt.next()
            S.dma(DMA(gt_[:, :, :, :], G.GT[:, :, t0:t0 + T3].rearrange("(i o) p t -> p i o t", i=3)), ("gt", gi),
                  writes=[r_gt])
            rs = [prefetch_res(S, C, x_in[t0 + s * 128:t0 + (s + 1) * 128, :]) for s in range(T3 // 128)]
            return (qa, r_qa, kw, r_kw, wa, r_wa, qm, r_qm, brb, r_brb, gt_, r_gt, rs) + segbuf[sidx]

        nxt_tile = load_tile(0)
        pending_end = []
        cur_gen = [None]
        for t in range(NT):
            t0 = t * T3
            kb = t0 // 128
            sidx = t0 // SEG
            (qa, r_qa, kw, r_kw, wa, r_wa, qm, r_qm, brb, r_brb, gt_, r_gt, rslots, km_, r_km, vm_, r_vm) = nxt_tile

            def a_score(job):
                nonlocal emi
                qb, g, n_, kt, nk, pvslot = job
                qabs = kb + qb
                i = kt - (kb - 1)
                off = kt - qabs
                same = (kt * 128) // SEG == (qabs * 128) // SEG
                ty = off + 1 if same else (3 if off == -1 else 4)
                sl_, r_sl, _ = SpL.next()
                su_, r_su, _ = SpU.next()
                sl2 = sl_[:, 0:256].rearrange("p (a q) -> p a q", a=2)
                su2 = su_[:, 0:256].rearrange("p (a q) -> p a q", a=2)
                qs = slice(qb * 128, (qb + 1) * 128)
                S.op("pe", MM([(sl2, kw[0:64, g, i * 128:(i + 1) * 128], qa[0:64, 2 * g:2 * g + 2, qs], True, True),
                               (su2, kw[64:128, g, i * 128:(i + 1) * 128], qa[64:128, 2 * g:2 * g + 2, qs], True, True)]),
                     reads=[r_kw, r_qa], writes=[r_sl, r_su])
                p_, r_p, _ = Pt.next()
                S.op("act", ACTF(p_[:, 0:256], sl_[:, 0:256], AF.Exp, scale=0.125), reads=[r_sl], writes=[r_p])
                S.op("act", ACTF(p_[:, 256:512], su_[:, 0:256], AF.Exp, scale=0.125), reads=[r_su], writes=[r_p])
                e1 = "dve" if emi % 2 == 0 else "pool"
                emi += 1
                S.op(e1, TT(p_[:, :], p_[:, :], EA[:, g * 5 + ty, :], ALU.mult), reads=[r_p], writes=[r_p])
                return (job, p_, r_p, i)

            def a_pv(st):
                job, p_, r_p, i = st
                qb, g, n_, kt, nk, pvslot = job
                pv_, r_pv, _ = pvslot
                pv4 = pv_[:, :].rearrange("p (a q) -> p a q", a=4)
                p4 = p_[:, :].rearrange("p (a q) -> p a q", a=4)
                qs = slice(qb * 128, (qb + 1) * 128)
                S.op("pe", MM([(pv4[:, 0:2, :], wa[:, i, g, 64:192], p4[:, 0:2, :], n_ == 0, False),
                               (pv4[:, 2:4, :], wa[:, i, g, 0:128], p4[:, 2:4, :], False, False, True)]),
                     reads=[r_p, r_wa], writes=[r_pv])
                if n_ == nk - 1:
                    S.op("pe", MM([(pv4[:, 0:2, :], sel[0:1, 0:128], esrow[0:1, g, 0, :, :], False, True),
                                   (pv4[:, 2:4, :], sel[0:1, 128:256], esrow[0:1, g, 1, :, :], False, False, True)]),
                         writes=[r_pv])

                    def fin(pv4=pv4, r_pv=r_pv, qb=qb, g=g, qs=qs):
                        ta, r_ta, _ = tmpA.next()
                        rl, ru = r_brA[qb * 2 + g]
                        recip_act(S, ta[0:64, 0:2, :], pv4[64:128, 0:2, :], [r_pv], r_ta)
                        recip_act(S, ta[64:128, 2:4, :], pv4[0:64, 2:4, :], [r_pv, r_ta], r_ta)
                        S.op("dve", TT(brA[0:64, 2 * g:2 * g + 2, qs], pv4[0:64, 0:2, :], ta[0:64, 0:2, :], ALU.mult),
                             reads=[r_pv, r_ta], writes=[rl])
                        S.op("dve", TT(brA[64:128, 2 * g:2 * g + 2, qs], pv4[64:128, 2:4, :], ta[64:128, 2:4, :], ALU.mult),
                             reads=[r_pv, r_ta], writes=[ru])
                    deferred.append([2, fin])

            jobs = []
            for qb in range(T3 // 128):
                qabs = kb + qb
                for g in range(2):
                    klist = [kt for kt in (qabs - 1, qabs, qabs + 1) if 0 <= kt < NKT]
                    pvslot = PVb.next()
                    for n_, kt in enumerate(klist):
                        jobs.append((qb, g, n_, kt, len(klist), pvslot))

            def m_score(h):
                sp_, r_sp, _ = SpL.next()
                sp2 = sp_[:, :].rearrange("p (a q) -> p a q", a=2)
                S.op("pe", MM([(sp2[:, mt, :], km_[:, h, mt * 128:(mt + 1) * 128], qm[:, h, :], True, True) for mt in range(2)]),
                     reads=[r_km, r_qm], writes=[r_sp])
                p_, r_p, _ = Pt.next()
                S.op("act", ACTF(p_[:, :], sp_[:, :], AF.Exp, scale=128.0 ** -0.5), reads=[r_sp], writes=[r_p])
                return (h, p_, r_p)

            def m_pv(st):
                h, p_, r_p = st
                p2 = p_[:, :].rearrange("p (a q) -> p a q", a=2)
                pv_, r_pv = pjx, r_pjx
                pv2 = pv_[:, :].rearrange("p (a q) -> p a q", a=2)
                S.op("pe", MM([(pv2[:, 0, :], vm_[:, mt, h * 128:(h + 1) * 128], p2[:, mt, :], mt == 0, mt == 1) for mt in range(2)]
                              + [(pv2[:, 1, :], ones[:, :], p2[:, mt, :], mt == 0, mt == 1) for mt in range(2)]),
                     reads=[r_p, r_vm], writes=[r_pv])
                tm, r_tm, _ = tmpM.next()
                recip_act(S, tm[:, :], pv2[:, 1, :], [r_pv], r_tm)
                S.op("dve", TT(brM[:, h, :], pv2[:, 0, :], tm[:, :], ALU.mult), reads=[r_pv, r_tm], writes=[r_brM[h]])

            seq = []
            mi = 0
            for n_, job in enumerate(jobs):
                seq.append(("a", job))
                if n_ % 3 == 1 and mi < 4:
                    seq.append(("m", mi))
                    mi += 1
            while mi < 4:
                seq.append(("m", mi))
                mi += 1
            pend = []
            deferred = []
            for si_, (kind, job) in enumerate(seq):
                pend.append((kind, a_score(job) if kind == "a" else m_score(job)))
                if len(pend) > 1:
                    kd, st_ = pend.pop(0)
                    (a_pv if kd == "a" else m_pv)(st_)
                for dfr in list(deferred):
                    dfr[0] -= 1
                    if dfr[0] < 0:
                        deferred.remove(dfr)
                        dfr[1]()
                if si_ >= 3 and (pending_end or cur_gen[0] is not None):
                    if cur_gen[0] is None:
                        cur_gen[0] = end_phase_b_gen(S, C, *pending_end.pop(0))
                    try:
                        next(cur_gen[0])
                    except StopIteration:
                        cur_gen[0] = None
            while pend:
                kd, st_ = pend.pop(0)
                (a_pv if kd == "a" else m_pv)(st_)
            for dfr in deferred:
                dfr[1]()
            while cur_gen[0] is not None or pending_end:
                if cur_gen[0] is None:
                    cur_gen[0] = end_phase_b_gen(S, C, *pending_end.pop(0))
                for _ in cur_gen[0]:
                    pass
                cur_gen[0] = None
            if t + 1 < NT:
                nxt_tile = load_tile(t + 1)
            r_brAall = [x for pr in r_brA[:2 * (T3 // 128)] for x in pr]
            for oc in range(8):
                pja, r_pja, pjb, r_pjb = pj_sets[oc % 2]
                pj0 = pja[:, :].rearrange("p (a q) -> p a q", a=2)
                pj1 = pjb
                r_pjs = [r_pja, r_pjb]
                for bi_, brt, rds, wr_ in ((1, brb, [r_brb], r_pjs[0]), (2, brM, r_brM, r_pjs[1]), (0, brA, r_brAall, r_pjs[0])):
                    o_ = pj0[:, bi_, :] if bi_ < 2 else pj1[:, 0:256]
                    S.op("pe", MM([(o_, Wbr[:, bi_, kc, oc * 128:(oc + 1) * 128], brt[:, kc, :], kc == 0, kc == 3)
                                   for kc in range(4)]), reads=rds, writes=[wr_])
                tg_, r_tg, _ = tg.next()
                S.op("dve", TT(tg_[:, 0:2, :], pj0[:, 0:2, :], gt_[:, 0:2, oc, :], ALU.mult), reads=[r_pjs[0], r_gt], writes=[r_tg])
                S.op("dve", TT(tg_[:, 2, :], pj1[:, 0:256], gt_[:, 2, oc, :], ALU.mult), reads=[r_pjs[1], r_gt], writes=[r_tg])
                S.op("pool", TT(tg_[:, 0, :], tg_[:, 0, :], tg_[:, 1, :], ALU.add), reads=[r_tg], writes=[r_tg])
                S.op("pool", TT(mixT[:, oc, :], tg_[:, 0, :], tg_[:, 2, :], ALU.add), reads=[r_tg], writes=[r_mix[oc]])
            for kpass in range(2):
                for s in range(T3 // 128):
                    yparts = [SpL.tiles[s % 2], SpU.tiles[s % 2]]
                    ryh = [SpL.res[s % 2], SpU.res[s % 2]]
                    for hh in range(2):
                        S.op("pe", MM([(yparts[hh][:, :], mixT[:, kc, s * 128:(s + 1) * 128],
                                        Wo[:, kc, hh * 512:(hh + 1) * 512], kc == 0, kc == 7)
                                       for kc in range(4 * kpass, 4 * kpass + 4)]),
                             reads=r_mix[4 * kpass:4 * kpass + 4], writes=[ryh[hh]])
            for s in range(T3 // 128):
                yparts = [SpL.tiles[s % 2], SpU.tiles[s % 2]]
                ryh = [SpL.res[s % 2], SpU.res[s % 2]]
                rows = slice(t0 + s * 128, t0 + (s + 1) * 128)
                end_phase_a(S, C, [yparts[0][:, :], yparts[1][:, :]], ryh, None, 1.0 / ALPHA, rslots[s])
                pending_end.append((x_out[rows, :], rslots[s]))
        while pending_end:
            end_phase_b(S, C, *pending_end.pop(0))
        S.barrier_and_emit()


_CACHE = {}


def kernel(x_prompt, x_sample, mem_prompt, mem_sample, ffn1_w_in, ffn1_w_out, ln1_g, ln1_b, w_in, w_mem_kv,
           sink_a, w_branch, w_out, ln2_g, ln2_b, ffn2_w_in, ffn2_w_out, ln3_g, ln3_b):
    f32 = lambda a: np.ascontiguousarray(np.asarray(a), dtype=np.float32)
    x_prompt, x_sample, mem_prompt, mem_sample = f32(x_prompt), f32(x_sample), f32(mem_prompt), f32(mem_sample)
    wts = dict(ffn1_w_in=f32(ffn1_w_in), ffn1_w_out=f32(ffn1_w_out), ln1_g=f32(ln1_g), ln1_b=f32(ln1_b),
               w_in=f32(w_in), w_mem_kv=f32(w_mem_kv), sink_a=f32(sink_a), w_branch=f32(w_branch), w_out=f32(w_out),
               ln2_g=f32(ln2_g), ln2_b=f32(ln2_b), ffn2_w_in=f32(ffn2_w_in), ffn2_w_out=f32(ffn2_w_out),
               ln3_g=f32(ln3_g), ln3_b=f32(ln3_b))
    if "nc" not in _CACHE:
        _CACHE["nc"] = build(nseg=4, depth=2)[0]
    nc = _CACHE["nc"]
    tabs = [host_tables(0), host_tables(1)]
    in_maps = []
    for c in range(8):
        if c < 4:
            xc = x_prompt[4 * c:4 * c + 4].reshape(4 * SEG, D)
            mc = mem_prompt[4 * c:4 * c + 4]
            J = 0
        else:
            xc = x_sample[c - 4]
            mc = np.ascontiguousarray(np.broadcast_to(mem_sample[c - 4][None], (4, 256, D)))
            J = 1
        m = {"x": xc, "mem": mc}
        m.update(wts)
        m.update(tabs[J])
        in_maps.append(m)
    res = run_bass_kernel_spmd(nc, in_maps, core_ids=list(range(8)))
    y_prompt = np.stack([res.results[c]["y"] for c in range(4)]).reshape(16, SEG, D)
    y_sample = np.stack([res.results[c]["y"] for c in range(4, 8)]).reshape(4, 4 * SEG, D)
    return (y_prompt.astype(np.float32), y_sample.astype(np.float32))
```
